# Optimizing a Trainium2 kernel written in Bass

```python
import math
import jax, jax.numpy as jnp
from jax import lax
import numpy as np

D_MODEL = 1024
BATCH = 4
SEQ = 4096
DEPTH = 1
DEC_BATCH = 2
DEC_SEQ = 16384
PAST_LEN = 128

MLA_HEADS = 8
QK_NOPE = 128
QK_ROPE = 64
V_HEAD = 128
Q_LORA = 384
KV_LORA = 256
ROPE_BASE = 10000.0
Q_BLOCK = 128
HY_WIDTH = 1024
HY_ORDER = 2
HY_DIRS = 2
FILT_BANDS = 16
FILT_EMB = 1 + 2 * FILT_BANDS
FILT_HID = 64
FILT_OUT_SCALE = 0.005
DECAY_FAST = 0.3
DECAY_SLOW = 1.5
DECAY_TARGET = 1e-2
DECAY_SHIFT = 0.05
MAX_DECAY = math.log(DECAY_TARGET) / DECAY_FAST
MIN_DECAY = math.log(DECAY_TARGET) / DECAY_SLOW
D_FF = 2816
IN_Q = Q_LORA
IN_KV = KV_LORA
IN_KR = QK_ROPE
IN_HY = 3 * HY_WIDTH
IN_GATE = 2 * D_MODEL
IN_COLS = IN_Q + IN_KV + IN_KR + IN_HY + IN_GATE
IN_SPLITS = (IN_Q, IN_Q + IN_KV, IN_Q + IN_KV + IN_KR, IN_Q + IN_KV + IN_KR + IN_HY)
DN_ALPHA = (2.0 * DEPTH) ** 0.25
DN_BETA = (8.0 * DEPTH) ** -0.25
LN_EPS = 1e-5
RMS_EPS = 1e-6

kernel_name = "hybrid_hyena_mla_deepnorm_encoder"


def layer_norm(x, g, b):
    xf = x.astype(jnp.float32)
    mu = jnp.mean(xf, axis=-1, keepdims=True)
    xc = xf - mu
    var = jnp.mean(xc * xc, axis=-1, keepdims=True)
    return (xc * lax.rsqrt(var + LN_EPS) * g.astype(jnp.float32) + b.astype(jnp.float32)).astype(x.dtype)


def rms_norm(x, g):
    xf = x.astype(jnp.float32)
    ms = jnp.mean(xf * xf, axis=-1, keepdims=True)
    return (xf * lax.rsqrt(ms + RMS_EPS) * g.astype(jnp.float32)).astype(x.dtype)


def dwconv3(u, w, b):
    up = jnp.pad(u, ((0, 0), (1, 1), (0, 0)))
    return up[:, :-2] * w[0] + up[:, 1:-1] * w[1] + up[:, 2:] * w[2] + b


def rope_tables(L):
    pos = jnp.arange(L, dtype=jnp.float32)
    inv = ROPE_BASE ** (-jnp.arange(0, QK_ROPE, 2, dtype=jnp.float32) / QK_ROPE)
    ang = pos[:, None] * inv[None, :]
    return jnp.cos(ang), jnp.sin(ang)


def apply_rope(x, cos, sin):
    x1, x2 = jnp.split(x, 2, axis=-1)
    return jnp.concatenate([x1 * cos - x2 * sin, x1 * sin + x2 * cos], axis=-1)


def mla_branch(c_q, c_kv, k_r, q_norm_g, w_uq, kv_norm_g, w_ukv, w_o_mla):
    B, L, _ = c_q.shape
    q = (rms_norm(c_q, q_norm_g) @ w_uq).reshape(B, L, MLA_HEADS, QK_NOPE + QK_ROPE)
    kv = (rms_norm(c_kv, kv_norm_g) @ w_ukv).reshape(B, L, MLA_HEADS, QK_NOPE + V_HEAD)
    q_nope, q_rope = q[..., :QK_NOPE], q[..., QK_NOPE:]
    k_nope, v = kv[..., :QK_NOPE], kv[..., QK_NOPE:]
    cos, sin = rope_tables(L)
    q_rope = apply_rope(q_rope, cos[None, :, None, :], sin[None, :, None, :])
    k_rope = apply_rope(k_r, cos[None], sin[None])
    scale = (QK_NOPE + QK_ROPE) ** -0.5
    nblk = L // Q_BLOCK
    qn_b = q_nope.reshape(B, nblk, Q_BLOCK, MLA_HEADS, QK_NOPE).transpose(1, 0, 2, 3, 4)
    qr_b = q_rope.reshape(B, nblk, Q_BLOCK, MLA_HEADS, QK_ROPE).transpose(1, 0, 2, 3, 4)

    def attend(blk):
        qn, qr = blk
        s = (jnp.einsum('bqhd,bkhd->bhqk', qn, k_nope).astype(jnp.float32)
             + jnp.einsum('bqhr,bkr->bhqk', qr, k_rope).astype(jnp.float32)) * scale
        p = jax.nn.softmax(s, axis=-1).astype(v.dtype)
        return jnp.einsum('bhqk,bkhd->bqhd', p, v)

    o = lax.map(attend, (qn_b, qr_b))
    o = o.transpose(1, 0, 2, 3, 4).reshape(B, L, MLA_HEADS * V_HEAD)
    return o @ w_o_mla


def hyena_filters(L, filt_w1, filt_b1, filt_freq, filt_w2, filt_b2, filt_w3):
    f32 = jnp.float32
    pos = jnp.arange(L, dtype=f32)
    t = pos / max(L - 1, 1)
    bands = jnp.linspace(1e-4, FILT_BANDS - 1, FILT_BANDS, dtype=f32)
    ang = (2.0 * math.pi * pos / L)[:, None] * bands[None, :]
    z = jnp.concatenate([t[:, None], jnp.cos(ang), -jnp.sin(ang)], axis=-1)
    freq = filt_freq.astype(f32)
    h = jnp.sin(freq * (z @ filt_w1.astype(f32) + filt_b1.astype(f32)))
    h = jnp.sin(freq * (h @ filt_w2.astype(f32) + filt_b2.astype(f32)))
    h = h @ filt_w3.astype(f32)
    deltas = jnp.abs(jnp.linspace(MIN_DECAY, MAX_DECAY, HY_WIDTH, dtype=f32))
    window = jnp.exp(-t[:, None] * deltas[None, :]) + DECAY_SHIFT
    return h.reshape(L, HY_DIRS, HY_ORDER, HY_WIDTH) * window[:, None, None, :]


def long_conv(v, h_fwd, h_bwd, skip):
    L, C = h_fwd.shape
    k = jnp.concatenate([h_fwd, jnp.zeros((1, C), jnp.float32), h_bwd[:0:-1]], axis=0)
    vf = jnp.fft.rfft(v, n=2 * L, axis=1)
    kf = jnp.fft.rfft(k, axis=0)
    y = jnp.fft.irfft(vf * kf[None], n=2 * L, axis=1)[:, :L]
    return y + v * skip.astype(jnp.float32)


def hyena_branch(u, short_w, short_b, filt_w1, filt_b1, filt_freq, filt_w2, filt_b2, filt_w3, hy_skip, w_o_hy):
    L = u.shape[1]
    u = dwconv3(u, short_w, short_b)
    x1, x2, v = jnp.split(u, 3, axis=-1)
    h = hyena_filters(L, filt_w1, filt_b1, filt_freq, filt_w2, filt_b2, filt_w3)
    z = v.astype(jnp.float32)
    for n, gate in enumerate((x1, x2)):
        z = gate.astype(jnp.float32) * long_conv(z, h[:, 0, n], h[:, 1, n], hy_skip[n])
    return z.astype(u.dtype) @ w_o_hy


def encoder_layer(x, w_in, short_w, short_b, q_norm_g, w_uq, kv_norm_g, w_ukv, w_o_mla,
                  filt_w1, filt_b1, filt_freq, filt_w2, filt_b2, filt_w3, hy_skip, w_o_hy,
                  w_out, ln1_g, ln1_b, w_ffn_up, dw_w, dw_b, w_ffn_down, ln2_g, ln2_b):
    proj = x @ w_in
    c_q, c_kv, k_r, u_hy, g = jnp.split(proj, IN_SPLITS, axis=-1)
    o_mla = mla_branch(c_q, c_kv, k_r, q_norm_g, w_uq, kv_norm_g, w_ukv, w_o_mla)
    o_hy = hyena_branch(u_hy, short_w, short_b, filt_w1, filt_b1, filt_freq, filt_w2, filt_b2,
                        filt_w3, hy_skip, w_o_hy)
    g_hy, g_mla = jnp.split(g, 2, axis=-1)
    merged = jax.nn.sigmoid(g_hy) * o_hy + jax.nn.sigmoid(g_mla) * o_mla
    x = layer_norm(DN_ALPHA * x + merged @ w_out, ln1_g, ln1_b)
    a, b = jnp.split(x @ w_ffn_up, 2, axis=-1)
    hmid = jax.nn.gelu(dwconv3(a, dw_w, dw_b), approximate=False) * b
    return layer_norm(DN_ALPHA * x + hmid @ w_ffn_down, ln2_g, ln2_b)


def setup_inputs(seed: int = 0) -> dict:
    key = jax.random.key(seed)
    ks = jax.random.split(key, 32)

    def nrm(k, shape, scale):
        return jax.random.normal(k, shape, jnp.float32) * scale

    def gain(k, shape):
        return 1.0 + nrm(k, shape, 0.02)

    Dp = DEPTH
    return {
        "x_prompt": nrm(ks[0], (BATCH, SEQ, D_MODEL), 1.0),
        "x_sample": nrm(ks[1], (DEC_BATCH, DEC_SEQ, D_MODEL), 1.0),
        "w_in": nrm(ks[2], (Dp, D_MODEL, IN_COLS), D_MODEL ** -0.5),
        "short_w": nrm(ks[3], (Dp, 3, IN_HY), 3 ** -0.5),
        "short_b": nrm(ks[4], (Dp, IN_HY), 0.02),
        "q_norm_g": gain(ks[5], (Dp, Q_LORA)),
        "w_uq": nrm(ks[6], (Dp, Q_LORA, MLA_HEADS * (QK_NOPE + QK_ROPE)), Q_LORA ** -0.5),
        "kv_norm_g": gain(ks[7], (Dp, KV_LORA)),
        "w_ukv": nrm(ks[8], (Dp, KV_LORA, MLA_HEADS * (QK_NOPE + V_HEAD)), KV_LORA ** -0.5),
        "w_o_mla": nrm(ks[9], (Dp, MLA_HEADS * V_HEAD, D_MODEL), (MLA_HEADS * V_HEAD) ** -0.5),
        "filt_w1": nrm(ks[10], (Dp, FILT_EMB, FILT_HID), FILT_EMB ** -0.5),
        "filt_b1": nrm(ks[11], (Dp, FILT_HID), 0.1),
        "filt_freq": gain(ks[12], (Dp, FILT_HID)),
        "filt_w2": nrm(ks[13], (Dp, FILT_HID, FILT_HID), FILT_HID ** -0.5),
        "filt_b2": nrm(ks[14], (Dp, FILT_HID), 0.1),
        "filt_w3": nrm(ks[15], (Dp, FILT_HID, HY_DIRS * HY_ORDER * HY_WIDTH), FILT_OUT_SCALE),
        "hy_skip": nrm(ks[16], (Dp, HY_ORDER, HY_WIDTH), 0.5),
        "w_o_hy": nrm(ks[17], (Dp, HY_WIDTH, D_MODEL), HY_WIDTH ** -0.5),
        "w_out": nrm(ks[18], (Dp, D_MODEL, D_MODEL), D_MODEL ** -0.5 * DN_BETA),
        "ln1_g": gain(ks[19], (Dp, D_MODEL)),
        "ln1_b": nrm(ks[20], (Dp, D_MODEL), 0.02),
        "w_ffn_up": nrm(ks[21], (Dp, D_MODEL, 2 * D_FF), D_MODEL ** -0.5),
        "dw_w": nrm(ks[22], (Dp, 3, D_FF), 3 ** -0.5),
        "dw_b": nrm(ks[23], (Dp, D_FF), 0.02),
        "w_ffn_down": nrm(ks[24], (Dp, D_FF, D_MODEL), D_FF ** -0.5 * DN_BETA),
        "ln2_g": gain(ks[25], (Dp, D_MODEL)),
        "ln2_b": nrm(ks[26], (Dp, D_MODEL), 0.02),
    }


def reference(x_prompt, x_sample, w_in, short_w, short_b, q_norm_g, w_uq, kv_norm_g, w_ukv, w_o_mla,
              filt_w1, filt_b1, filt_freq, filt_w2, filt_b2, filt_w3, hy_skip, w_o_hy,
              w_out, ln1_g, ln1_b, w_ffn_up, dw_w, dw_b, w_ffn_down, ln2_g, ln2_b):
    y_prompt = x_prompt
    y_sample = x_sample
    for i in range(DEPTH):
        lp = (w_in[i], short_w[i], short_b[i], q_norm_g[i], w_uq[i], kv_norm_g[i], w_ukv[i], w_o_mla[i],
              filt_w1[i], filt_b1[i], filt_freq[i], filt_w2[i], filt_b2[i], filt_w3[i], hy_skip[i], w_o_hy[i],
              w_out[i], ln1_g[i], ln1_b[i], w_ffn_up[i], dw_w[i], dw_b[i], w_ffn_down[i], ln2_g[i], ln2_b[i])
        y_prompt = encoder_layer(y_prompt, *lp)
        y_sample = encoder_layer(y_sample, *lp)
    return (y_prompt, y_sample)
```

```python
import math
from contextlib import ExitStack

import numpy as np
import concourse.bass as bass
import concourse.mybir as mybir
from concourse.bass_utils import run_bass_kernel_spmd

F32 = mybir.dt.float32
BF16 = mybir.dt.bfloat16
AF = mybir.ActivationFunctionType
ALU = mybir.AluOpType

ENGS = ("pe", "act", "dve", "pool", "sp")
NDMA_SEM = 24

D = 1024
NH = 8
QL = 384
KVL = 256
KR = 64
DFF = 2816
NF = DFF // 128
IN_KV0 = 384
IN_KR0 = 640
IN_HY0 = 704
IN_G0 = 3776
ALPHA = 2.0 ** 0.25
LN_EPS = 1e-5
RMS_EPS = 1e-6
ATT_SCALE = 192.0 ** -0.5
MAGIC = 12582912.0
TWO_PI = 2.0 * math.pi

GROUPS = [
    dict(g=0, L=16384, N1=128, NQ=4096, JI=32, Jc=16),
    dict(g=1, L=4096, N1=32, NQ=2048, JI=128, Jc=64),
]
for _G in GROUPS:
    _G["NQX"] = _G["NQ"] + 2
    _G["NSEL"] = _G["NQ"] // _G["N1"] + 2
    _G["NB"] = _G["N1"] + 2
    _G["cpk"] = 128 // _G["N1"]
    _G["NCI"] = 1024 // _G["JI"]
    _G["NSC"] = 1024 // _G["Jc"]
    _G["Jf"] = 2 * _G["Jc"]
    _G["NP"] = _G["N1"] * 128


class Op:
    __slots__ = ("eng", "fn", "r", "w", "dma", "waits", "dmawaits", "marked", "didx", "ie", "cnt")

    def __init__(self, eng, fn, r, w, dma):
        self.eng, self.fn, self.r, self.w, self.dma = eng, fn, tuple(r), tuple(w), dma
        self.waits = {}
        self.dmawaits = []
        self.marked = False
        self.didx = None
        self.ie = -1
        self.cnt = 0


class Prog:
    def __init__(self):
        self.ops = []

    def op(self, eng, fn, r=(), w=()):
        self.ops.append(Op(eng, fn, r, w, False))

    def dma(self, fn, r=(), w=(), eng="sp"):
        self.ops.append(Op(eng, fn, r, w, True))

    def barrier(self):
        self.ops.append(Op(None, None, (), (), False))

    def analyze(self):
        last_w, readers = {}, {}
        n_on = {e: 0 for e in ENGS}
        last_on = {e: None for e in ENGS}
        recent_dma = []
        bar = None
        ndma = 0
        for o in self.ops:
            if o.eng is None:
                bar = [dict(last_on), list(recent_dma[-NDMA_SEM:]), set(ENGS)]
                last_w, readers = {}, {}
                continue
            deps = []
            if bar is not None and o.eng in bar[2]:
                bar[2].discard(o.eng)
                deps.extend(p for p in bar[0].values() if p is not None)
                deps.extend(bar[1])
            for k in o.r:
                p = last_w.get(k)
                if p is not None:
                    deps.append(p)
            for k in o.w:
                p = last_w.get(k)
                if p is not None:
                    deps.append(p)
                deps.extend(readers.get(k, ()))
            for p in deps:
                if p is o:
                    continue
                if p.dma:
                    if p not in o.dmawaits:
                        o.dmawaits.append(p)
                else:
                    if p.eng == "pe" and o.eng == "pe" and not o.dma:
                        continue
                    cur = o.waits.get(p.eng)
                    if cur is None or p.ie > cur.ie:
                        o.waits[p.eng] = p
            for k in o.w:
                last_w[k] = o
                readers[k] = []
            for k in o.r:
                readers.setdefault(k, []).append(o)
            o.ie = n_on[o.eng]
            n_on[o.eng] += 1
            if not o.dma:
                last_on[o.eng] = o
            else:
                o.didx = ndma
                ndma += 1
                recent_dma.append(o)
        for o in self.ops:
            if o.eng is None:
                continue
            for p in o.waits.values():
                p.marked = True
        cnt = {e: 0 for e in ENGS}
        for o in self.ops:
            if o.eng is None or o.dma:
                continue
            if o.marked:
                cnt[o.eng] += 1
            o.cnt = cnt[o.eng]
        return cnt

    def emit(self, sems, dsems, block):
        streams = {e: [] for e in ENGS}
        known = {e: {f: 0 for f in ENGS} for e in ENGS}
        kd = {e: [0] * NDMA_SEM for e in ENGS}
        dma_ops = [o for o in self.ops if o.eng is not None and o.dma]
        for o in self.ops:
            if o.eng is None:
                continue
            e = o.eng
            st = streams[e]
            for f, p in o.waits.items():
                if p.cnt > known[e][f]:
                    known[e][f] = p.cnt
                    st.append(("w", sems[f], p.cnt))
            dws = list(o.dmawaits)
            if o.dma and o.didx >= NDMA_SEM:
                dws.append(dma_ops[o.didx - NDMA_SEM])
            for p in dws:
                slot, val = p.didx % NDMA_SEM, 16 * (p.didx // NDMA_SEM + 1)
                if val > kd[e][slot]:
                    kd[e][slot] = val
                    st.append(("w", dsems[slot], val))
            if o.dma:
                st.append(("i", o.fn, dsems[o.didx % NDMA_SEM], 16))
            elif o.marked:
                st.append(("i", o.fn, sems[e], 1))
            else:
                st.append(("i", o.fn, None, 0))
        for p in dma_ops[-NDMA_SEM:]:
            slot, val = p.didx % NDMA_SEM, 16 * (p.didx // NDMA_SEM + 1)
            if val > kd["sp"][slot]:
                kd["sp"][slot] = val
                streams["sp"].append(("w", dsems[slot], val))

        def run(eng_obj, st):
            for it in st:
                if it[0] == "w":
                    eng_obj.wait_ge(it[1], it[2])
                else:
                    ins = it[1](eng_obj)
                    if it[2] is not None:
                        ins.then_inc(it[2], it[3])

        @block.tensor
        def _(t):
            run(t, streams["pe"])

        @block.scalar
        def _(t):
            run(t, streams["act"])

        @block.vector
        def _(t):
            run(t, streams["dve"])

        @block.gpsimd
        def _(t):
            run(t, streams["pool"])

        @block.sync
        def _(t):
            run(t, streams["sp"])

        return {e: len(s) for e, s in streams.items()}


def _fft_tables(L, N1):
    N = 2 * L
    m = np.arange(128)
    kh = np.arange(128) + 0.5
    T = {}
    f2 = np.zeros((128, 2, 256))
    for ch in range(2):
        n2 = 128 * ch + m
        ang = TWO_PI * np.outer(n2, kh) / 256.0
        f2[:, ch, 0:128] = np.cos(ang)
        f2[:, ch, 128:256] = -np.sin(ang)
    T["f2"] = f2.reshape(128, 512)
    q = np.arange(128)
    n1 = q % N1
    c4 = q // N1
    ang = TWO_PI * np.outer(n1, kh) / N
    twr, twi = np.cos(ang), -np.sin(ang)
    T["twa"] = np.concatenate([twr, twr, twr, twr], axis=1)
    T["twb"] = np.concatenate([twi, twi, twi, twi], axis=1)
    ang = TWO_PI * np.outer(n1, n1) / N1
    same = (c4[:, None] == c4[None, :]).astype(np.float64)
    C = np.cos(ang) * same
    S = np.sin(ang) * same
    T["f1"] = np.concatenate([C, S, -S, -C], axis=1)
    T["r12"] = np.concatenate([C, S, -S, C, -C, -S], axis=1)
    ang = TWO_PI * np.outer(kh, n1) / N
    itr, iti = np.cos(ang), np.sin(ang)
    T["ita"] = np.concatenate([itr, itr, itr, itr], axis=1)
    T["itb"] = np.concatenate([iti, iti, iti, iti], axis=1)
    ang = TWO_PI * np.outer(kh, m) / 256.0
    T["ics"] = np.concatenate([np.cos(ang) * (2.0 / N), -np.sin(ang) * (2.0 / N), -np.cos(ang) * (2.0 / N)], axis=1)
    return {k: np.ascontiguousarray(v, dtype=np.float32) for k, v in T.items()}


def _filter_tables(L, N1):
    N = 2 * L
    NP = N1 * 128
    n1 = np.arange(N1)
    m = np.arange(128)
    pos = np.zeros((2, N1, 128))
    for ch in range(2):
        n = n1[:, None] + N1 * (128 * ch + m[None, :])
        pos[ch] = n if ch == 0 else (N - n)
    pos[1, 0, 0] = 0.0
    p = pos.reshape(-1)
    t = p / max(L - 1, 1)
    bands = np.linspace(1e-4, 15.0, 16)
    ang = (TWO_PI * p / L)[:, None] * bands[None, :]
    z = np.concatenate([t[:, None], np.cos(ang), -np.sin(ang)], axis=1)
    zT = np.ascontiguousarray(z.T, dtype=np.float32)
    min_decay = math.log(1e-2) / 1.5
    max_decay = math.log(1e-2) / 0.3
    deltas = np.abs(np.linspace(min_decay, max_decay, 1024))
    tneg = np.zeros((128, 2, N1), np.float32)
    for ch in range(2):
        tneg[:, ch, :] = -(pos[ch].T) / (L - 1.0)
    return zT, deltas.astype(np.float32)[None, :], tneg.reshape(128, 2 * N1)


def _rope_tables(positions):
    pos = positions.astype(np.float32)
    inv = (np.float32(10000.0) ** (-(np.arange(0, 64, 2, dtype=np.float32)) / np.float32(64))).astype(np.float32)
    ang = (pos[:, None] * inv[None, :]).astype(np.float32)
    c = np.cos(ang).T
    s = np.sin(ang).T
    return np.ascontiguousarray(np.concatenate([c, c, s, s], axis=0), dtype=np.float32)


class Builder:
    def __init__(self, groups=(0, 1), dbg=False):
        self.nc = bass.Bass("TRN2", target_bir_lowering=False)
        self.P = Prog()
        self.groups = groups
        self.dbg = dbg
        self.din = {}
        self.dout = {}
        self.uid = 0

    def inp(self, name, shape, dt=F32):
        self.din[name] = self.nc.dram_tensor(name, list(shape), dt, kind="ExternalInput").ap()
        return self.din[name]

    def outp(self, name, shape, dt=F32):
        self.dout[name] = self.nc.dram_tensor(name, list(shape), dt, kind="ExternalOutput").ap()
        return self.dout[name]

    def scratch(self, name, shape, dt):
        return self.nc.dram_tensor(name, list(shape), dt).ap()

    def sb(self, es, name, shape, dt):
        self.uid += 1
        return es.enter_context(self.nc.sbuf_tensor(f"{name}_{self.uid}", list(shape), dt))

    def psum(self, es, name, shape, dt):
        self.uid += 1
        return es.enter_context(self.nc.psum_tensor(f"{name}_{self.uid}", list(shape), dt))

    def mm(self, out, lhsT, rhs, start, stop, r, w):
        self.P.op("pe", lambda e: e.matmul(out, lhsT=lhsT, rhs=rhs, start=start, stop=stop), r=r, w=w)

    def dma(self, out, in_, r, w, slow=False):
        if slow:
            self.P.dma(lambda e: e.dma_start(out=out, in_=in_, allow_slow_non_contiguous=True), r=r, w=w)
        else:
            self.P.dma(lambda e: e.dma_start(out=out, in_=in_), r=r, w=w)

    def tt(self, eng, out, in0, in1, op, r, w):
        self.P.op(eng, lambda e: e.tensor_tensor(out=out, in0=in0, in1=in1, op=op), r=r, w=w)

    def ts(self, eng, out, in0, s1, s2, op0, op1, r, w):
        if op1 is None:
            self.P.op(eng, lambda e: e.tensor_scalar(out=out, in0=in0, scalar1=s1, scalar2=None, op0=op0), r=r, w=w)
        else:
            self.P.op(eng, lambda e: e.tensor_scalar(out=out, in0=in0, scalar1=s1, scalar2=s2, op0=op0, op1=op1), r=r, w=w)

    def stt(self, eng, out, in0, scalar, in1, op0, op1, r, w):
        self.P.op(eng, lambda e: e.scalar_tensor_tensor(out=out, in0=in0, scalar=scalar, in1=in1, op0=op0, op1=op1), r=r, w=w)

    def act(self, out, in_, func, r, w, bias=None, scale=None):
        kw = {}
        if bias is not None:
            kw["bias"] = bias
        if scale is not None:
            kw["scale"] = scale
        self.P.op("act", lambda e: e.activation(out=out, in_=in_, func=func, **kw), r=r, w=w)

    def copy(self, eng, out, in_, r, w):
        if eng == "act":
            self.P.op("act", lambda e: e.copy(out=out, in_=in_), r=r, w=w)
        else:
            self.P.op(eng, lambda e: e.tensor_copy(out=out, in_=in_), r=r, w=w)

    def memset(self, eng, ap, val, w):
        self.P.op(eng, lambda e: e.memset(ap, val), w=w)

    def load_w(self, dst, src, nk, ncols, key, off=0, eng="pool"):
        for kc in range(nk):
            c0 = 0
            while c0 < ncols:
                wdt = min(1024, ncols - c0)
                s = self.stg_i % 2
                self.stg_i += 1
                st = self.stg[s]
                self.dma(st[:, 0:wdt], src[kc * 128:(kc + 1) * 128, c0:c0 + wdt], r=[], w=[("stg", s)])
                self.copy(eng, dst[:, kc, off + c0:off + c0 + wdt], st[:, 0:wdt], r=[("stg", s)], w=[key])
                c0 += wdt

    def layer_norm(self, src, out, n, gbc, bbc, key_src, key_out, tmpk):
        st, mv = self.ln_st, self.ln_mv
        P = self.P
        for c in range(2):
            P.op("dve", lambda e, c=c: e.bn_stats(out=st[:n, c * 6:(c + 1) * 6], in_=src[:n, c * 512:(c + 1) * 512]), r=[key_src], w=["ln_st"])
        P.op("dve", lambda e: e.bn_aggr(out=mv[:n, 0:2], in_=st[:n, 0:12]), r=["ln_st"], w=["ln_mv"])
        self.act(mv[:n, 2:3], mv[:n, 1:2], AF.Sqrt, r=["ln_mv", "epsc"], w=["ln_mv2"], bias=self.epsc[:n, 0:1], scale=1.0)
        P.op("dve", lambda e: e.reciprocal(out=mv[:n, 3:4], in_=mv[:n, 2:3]), r=["ln_mv2"], w=["ln_mv3"])
        self.ts("dve", src[:n, :], src[:n, :], mv[:n, 0:1], mv[:n, 3:4], ALU.subtract, ALU.mult, r=[key_src, "ln_mv", "ln_mv3"], w=[key_src])
        self.tt("pool", src[:n, :], src[:n, :], gbc[:n, :], ALU.mult, r=[key_src, "lnp"], w=[key_src])
        self.tt("pool", out[:n, :], src[:n, :], bbc[:n, :], ALU.add, r=[key_src, "lnp"], w=[key_out])

    def build(self):
        nc, P = self.nc, self.P
        W = {}
        for name, shape in [
            ("w_in", (1, 1024, 5824)), ("short_w", (1, 3, 3072)), ("short_b", (1, 3072)), ("q_norm_g", (1, 384)),
            ("w_uq", (1, 384, 1536)), ("kv_norm_g", (1, 256)), ("w_ukv", (1, 256, 2048)), ("w_o_mla", (1, 1024, 1024)),
            ("filt_w1", (1, 33, 64)), ("filt_b1", (1, 64)), ("filt_freq", (1, 64)), ("filt_w2", (1, 64, 64)),
            ("filt_b2", (1, 64)), ("filt_w3", (1, 64, 4096)), ("hy_skip", (1, 2, 1024)), ("w_o_hy", (1, 1024, 1024)),
            ("w_out", (1, 1024, 1024)), ("ln1_g", (1, 1024)), ("ln1_b", (1, 1024)), ("w_ffn_up", (1, 1024, 5632)),
            ("dw_w", (1, 3, 2816)), ("dw_b", (1, 2816)), ("w_ffn_down", (1, 2816, 1024)), ("ln2_g", (1, 1024)), ("ln2_b", (1, 1024)),
        ]:
            W[name] = self.inp(name, shape)
        self.W = W
        C = {}
        C["ident"] = self.inp("c_ident", (128, 128))
        C["stack2"] = self.inp("c_stack2", (128, 128))
        C["delta"] = self.inp("c_delta", (1, 1024))
        GI = {}
        for G in GROUPS:
            g = G["g"]
            d = {}
            d["xtp"] = self.inp(f"xtp{g}", (1024, G["NB"] * 128))
            d["xqT"] = self.inp(f"xqT{g}", (1024, G["NQX"]))
            d["xq"] = self.inp(f"xq{g}", (G["NQX"], 1024))
            d["ropek"] = self.inp(f"ropek{g}", (128, G["L"]))
            d["ropeq"] = self.inp(f"ropeq{g}", (128, G["NQX"]))
            d["zemb"] = self.inp(f"zemb{g}", (33, 2 * G["NP"]))
            d["tneg"] = self.inp(f"tneg{g}", (128, 2 * G["N1"]))
            d["sel"] = self.inp(f"sel{g}", (128, G["NSEL"]))
            d["hm"] = self.inp(f"hm{g}", (128, 2))
            for k, n in [("f2", 512), ("twa", 512), ("twb", 512), ("f1", 512), ("r12", 768), ("ita", 512), ("itb", 512), ("ics", 384)]:
                d[k] = self.inp(f"{k}{g}", (128, n))
            d["y"] = self.outp(f"y{g}", (G["NQ"], 1024))
            d["xtb"] = self.scratch(f"s_xtb{g}", (1024, G["NB"] * 128), BF16)
            d["kf"] = self.scratch(f"s_kf{g}", (G["NSC"] * 2 * 128, 16 * 256), BF16)
            d["ot"] = self.scratch(f"s_ot{g}", (1024, G["NQX"]), BF16)
            d["kts"] = self.scratch(f"s_kts{g}", (1024, G["L"]), BF16)
            d["vts"] = self.scratch(f"s_vts{g}", (1024, G["L"]), BF16)
            d["zt"] = self.scratch(f"s_zt{g}", (1024, G["NQX"]), BF16)
            d["y1s"] = self.scratch(f"s_y1{g}", (G["NQX"], 1024), F32)
            d["y1t"] = self.scratch(f"s_y1t{g}", (1024, G["NQX"]), BF16)
            GI[g] = d
        self.GI = GI
        self.wupb = self.scratch("s_wup", (1024, 5632), BF16)
        self.wdnb = self.scratch("s_wdn", (2816, 1024), BF16)
        self.whyb = self.scratch("s_why", (1024, 3072), BF16)

        with ExitStack() as es:
            self.sems = {e: es.enter_context(nc.semaphore(f"s_{e}")) for e in ENGS}
            self.dsems = [es.enter_context(nc.semaphore(f"d_{i}")) for i in range(NDMA_SEM)]
            self.stg = [self.sb(es, f"stg{i}", (128, 1024), F32) for i in range(2)]
            self.stg_i = 0
            self.identb = self.sb(es, "identb", (128, 128), BF16)
            self.identf = self.sb(es, "identf", (128, 128), F32)
            self.stack2 = self.sb(es, "stack2", (128, 128), BF16)
            self.ones = self.sb(es, "ones", (128, 128), BF16)
            self.epsc = self.sb(es, "epsc", (128, 2), F32)
            self.ln_st = self.sb(es, "ln_st", (128, 12), F32)
            self.ln_mv = self.sb(es, "ln_mv", (128, 4), F32)
            self.dma(self.identf[:], C["ident"], r=[], w=["identf"])
            self.copy("pool", self.identb[:], self.identf[:], r=["identf"], w=["identb"])
            self.dma(self.stg[0][:, 0:128], C["stack2"], r=[], w=[("stg", 0)])
            self.copy("pool", self.stack2[:], self.stg[0][:, 0:128], r=[("stg", 0)], w=["stack2"])
            self.memset("pool", self.ones[:], 1.0, w=["ones"])
            self.memset("pool", self.epsc[:, 0:1], LN_EPS, w=["epsc"])
            self.memset("pool", self.epsc[:, 1:2], RMS_EPS, w=["epsc"])

            self.phase_w()
            for G in GROUPS:
                if G["g"] not in self.groups:
                    continue
                self.phase_filters(G)
                self.phase_attn(G)
                self.phase_hyena(G)
                self.phase_c1(G)
            block = es.enter_context(nc.Block())
            cnt = P.analyze()
            n = P.emit(self.sems, self.dsems, block)
            self.stats = (cnt, n)
        return nc

    def phase_w(self):
        P = self.P
        with ExitStack() as es:
            wb = [self.sb(es, f"wcast{i}", (128, 1024), BF16) for i in range(2)]
            i = 0
            for (src, dst, rows, cols) in [(self.W["w_in"][0][:, IN_HY0:IN_HY0 + 3072], self.whyb, 1024, 3072),
                                           (self.W["w_ffn_up"][0], self.wupb, 1024, 5632), (self.W["w_ffn_down"][0], self.wdnb, 2816, 1024)]:
                for r0 in range(0, rows, 128):
                    c0 = 0
                    while c0 < cols:
                        wdt = min(1024, cols - c0)
                        s = self.stg_i % 2
                        self.stg_i += 1
                        b = i % 2
                        i += 1
                        self.dma(self.stg[s][:, 0:wdt], src[r0:r0 + 128, c0:c0 + wdt], r=[], w=[("stg", s)])
                        self.copy("pool" if b else "act", wb[b][:, 0:wdt], self.stg[s][:, 0:wdt], r=[("stg", s)], w=[("wcast", b)])
                        self.dma(dst[r0:r0 + 128, c0:c0 + wdt], wb[b][:, 0:wdt], r=[("wcast", b)], w=["wscr"])
                        c0 += wdt
            P.barrier()

    def load_fft_tables(self, es, G):
        d = self.GI[G["g"]]
        T = {}
        T["f2"] = self.sb(es, "t_f2", (128, 2, 256), BF16)
        T["twa"] = self.sb(es, "t_twa", (128, 512), F32)
        T["twb"] = self.sb(es, "t_twb", (128, 512), F32)
        T["f1"] = self.sb(es, "t_f1", (128, 4, 128), BF16)
        T["r12"] = self.sb(es, "t_r12", (128, 3, 256), BF16)
        T["ita"] = self.sb(es, "t_ita", (128, 512), F32)
        T["itb"] = self.sb(es, "t_itb", (128, 512), F32)
        T["ics"] = self.sb(es, "t_ics", (128, 3, 128), BF16)
        for k in ("twa", "twb", "ita", "itb"):
            self.dma(T[k][:], d[k], r=[], w=["ffttab"])
        for k, n in (("f2", 512), ("f1", 512), ("r12", 768), ("ics", 384)):
            s = self.stg_i % 2
            self.stg_i += 1
            self.dma(self.stg[s][:, 0:n], d[k], r=[], w=[("stg", s)])
            self.copy("pool", T[k][:].rearrange("p a b -> p (a b)"), self.stg[s][:, 0:n], r=[("stg", s)], w=["ffttab"])
        return T

    def fft_bufs(self, es, inverse):
        Bf = {}
        Bf["ta"] = [self.sb(es, f"f_ta{i}", (128, 512), BF16) for i in range(2)]
        Bf["tb"] = [self.sb(es, f"f_tb{i}", (128, 512), BF16) for i in range(2)]
        Bf["araw"] = [self.sb(es, f"f_araw{i}", (128, 512), BF16) for i in range(2)]
        if inverse:
            Bf["tai"] = [self.sb(es, f"f_tai{i}", (128, 512), BF16) for i in range(2)]
            Bf["tbi"] = [self.sb(es, f"f_tbi{i}", (128, 512), BF16) for i in range(2)]
            Bf["e"] = [[self.sb(es, f"f_e{i}_{k}", (128, 2, 128), BF16) for k in range(4)] for i in range(2)]
            Bf["g"] = [[self.sb(es, f"f_g{i}_{k}", (128, 256), F32) for k in range(2)] for i in range(2)]
        return Bf

    @staticmethod
    def skew_emit(items):
        ns = max(len(st) for st in items) if items else 0
        n = len(items)
        for t in range(n + ns - 1):
            for sidx in range(ns - 1, -1, -1):
                i = t - sidx
                if 0 <= i < n and sidx < len(items[i]) and items[i][sidx] is not None:
                    items[i][sidx]()

    def fwd_stages(self, T, Bf, bank, ii, x_of, nch2, xkeys):
        s2 = ii % 2
        b1, b3 = bank(0), bank(2 + ii % 2)
        k1, k3 = ("pb", 0), ("pb", 2 + ii % 2)
        ta, tb = Bf["ta"][s2], Bf["tb"][s2]
        araw = Bf["araw"][s2]

        def F0():
            for u in range(2):
                for ch2 in range(nch2):
                    self.mm(b1[:, u * 256:(u + 1) * 256], x_of(u, ch2), T["f2"][:, ch2, :], ch2 == 0, ch2 == nch2 - 1,
                            r=list(xkeys) + ["ffttab"], w=[k1])

        def F0c():
            self.copy("act", araw[:], b1, r=[k1], w=[("araw", s2)])

        def F1():
            self.tt("dve", ta[:], araw[:], T["twa"][:], ALU.mult, r=[("araw", s2), "ffttab"], w=[("ta", s2)])
            self.tt("dve", tb[:], araw[:], T["twb"][:], ALU.mult, r=[("araw", s2), "ffttab"], w=[("tb", s2)])

        def F3():
            ta4 = ta[:].rearrange("p (u r k) -> p u r k", u=2, r=2)
            tb4 = tb[:].rearrange("p (u r k) -> p u r k", u=2, r=2)
            Cm, Sm, nSm, nCm = (T["f1"][:, i, :] for i in range(4))
            rr = [("ta", s2), ("tb", s2), "ffttab"]
            o = b3[:, 0:256]
            self.mm(o, Cm, ta4[:, :, 0, :], True, False, r=rr, w=[k3])
            self.mm(o, nCm, tb4[:, :, 1, :], False, False, r=rr, w=[k3])
            self.mm(o, Sm, tb4[:, :, 0, :], False, False, r=rr, w=[k3])
            self.mm(o, Sm, ta4[:, :, 1, :], False, True, r=rr, w=[k3])
            o = b3[:, 256:512]
            self.mm(o, Cm, tb4[:, :, 0, :], True, False, r=rr, w=[k3])
            self.mm(o, Cm, ta4[:, :, 1, :], False, False, r=rr, w=[k3])
            self.mm(o, nSm, ta4[:, :, 0, :], False, False, r=rr, w=[k3])
            self.mm(o, Sm, tb4[:, :, 1, :], False, True, r=rr, w=[k3])

        return [F0, F0c, F1, F3], b3, k3

    def inv_stages(self, T, Bf, bank, ii, b3, k3, Kp, kkey, cur_pair, cur_key, gate_pair, gate_key, skip_bc, dst_pair, dst_key, N1, skey):
        s2 = ii % 2
        b5, b7 = bank(4 + ii % 2), bank(6)
        k5, k7 = ("pb", 4 + ii % 2), ("pb", 6)
        e = Bf["e"][s2]
        g = Bf["g"][s2]
        tai, tbi = Bf["tai"][s2], Bf["tbi"][s2]
        Xr = b3[:, 0:256].rearrange("p (u k) -> p u k", u=2)
        Xi = b3[:, 256:512].rearrange("p (u k) -> p u k", u=2)
        Kr, Ki = Kp[:, :, 0, :], Kp[:, :, 1, :]
        ek = [("e", s2, k) for k in range(4)]

        def F4():
            self.tt("dve", e[0][:], Xr, Kr, ALU.mult, r=[k3, kkey], w=[ek[0]])
            self.tt("dve", e[1][:], Xi, Ki, ALU.mult, r=[k3, kkey], w=[ek[1]])
            self.tt("dve", e[2][:], Xr, Ki, ALU.mult, r=[k3, kkey], w=[ek[2]])
            self.tt("dve", e[3][:], Xi, Kr, ALU.mult, r=[k3, kkey], w=[ek[3]])

        def I0():
            R1, R2, R3 = (T["r12"][:, i, :] for i in range(3))
            for u in range(2):
                o = b5[:, u * 256:(u + 1) * 256]
                self.mm(o, e[0][:, u, :], R1, True, False, r=ek + ["ffttab"], w=[k5])
                self.mm(o, e[1][:, u, :], R3, False, False, r=ek + ["ffttab"], w=[k5])
                self.mm(o, e[2][:, u, :], R2, False, False, r=ek + ["ffttab"], w=[k5])
                self.mm(o, e[3][:, u, :], R2, False, True, r=ek + ["ffttab"], w=[k5])

        def I1():
            self.tt("dve", tai[:], b5, T["ita"][:], ALU.mult, r=[k5, "ffttab"], w=[("tai", s2)])
            self.tt("dve", tbi[:], b5, T["itb"][:], ALU.mult, r=[k5, "ffttab"], w=[("tbi", s2)])

        def I3():
            ta4 = tai[:].rearrange("p (u r k) -> p u r k", u=2, r=2)
            tb4 = tbi[:].rearrange("p (u r k) -> p u r k", u=2, r=2)
            IC, ISn, nIC = (T["ics"][:, i, :] for i in range(3))
            rr = [("tai", s2), ("tbi", s2), "ffttab"]
            o = b7[:, 0:256]
            self.mm(o, IC, ta4[:, :, 0, :], True, False, r=rr, w=[k7])
            self.mm(o, nIC, tb4[:, :, 1, :], False, False, r=rr, w=[k7])
            self.mm(o, ISn, tb4[:, :, 0, :], False, False, r=rr, w=[k7])
            self.mm(o, ISn, ta4[:, :, 1, :], False, True, r=rr, w=[k7])

        def I4():
            self.tt("pool", g[0][:].rearrange("p (c n) -> p c n", n=N1), cur_pair.rearrange("p (c n) -> p c n", n=N1), skip_bc, ALU.mult,
                    r=[cur_key, skey], w=[("g", s2, 0)])
            self.tt("dve", g[1][:], b7[:, 0:256], g[0][:], ALU.add, r=[k7, ("g", s2, 0)], w=[("g", s2, 1)])
            self.tt("pool", dst_pair, g[1][:], gate_pair, ALU.mult, r=[("g", s2, 1), gate_key], w=[dst_key])

        return [F4, I0, I1, I3, I4]

    def phase_filters(self, G):
        P = self.P
        g, N1, L, NP, Jc, NSC = G["g"], G["N1"], G["L"], G["NP"], G["Jc"], G["NSC"]
        d = self.GI[g]
        W = self.W
        with ExitStack() as es:
            T = self.load_fft_tables(es, G)
            ps = self.psum(es, "ps0", (128, 4096), F32)
            bank = lambda i: ps[:, i * 512:(i + 1) * 512]
            Bf = self.fft_bufs(es, inverse=False)
            h2T = self.sb(es, "h2T", (128, NP), BF16)
            w1 = self.sb(es, "fw1", (33, 128), F32)
            w2 = self.sb(es, "fw2", (128, 128), F32)
            fp = self.sb(es, "fpar", (128, 8), F32)
            w3r = self.sb(es, "w3r", (128, 4096), BF16)
            w3s = self.sb(es, "w3s", (128, 2, NSC, 2, Jc), BF16)
            delta = self.sb(es, "delta", (128, 1024), F32)
            tneg = self.sb(es, "tneg", (128, 2, N1), F32)
            zst = [self.sb(es, f"zst{i}", (33, 512), F32) for i in range(2)]
            hA = self.sb(es, "hA", (128, 512), F32)
            hB = self.sb(es, "hB", (128, 512), F32)
            h1 = self.sb(es, "h1", (128, 512), F32)
            nb1 = 512 // (2 * Jc)
            argt = [self.sb(es, f"argt{i}", (128, nb1, Jc), F32) for i in range(2)]
            e2t = [self.sb(es, f"e2t{i}", (128, nb1, Jc), F32) for i in range(2)]
            kt = [self.sb(es, f"kt{i}", (128, 2, 2, Jc * N1), BF16) for i in range(2)]
            kfo = [self.sb(es, f"kfo{i}", (128, 16, 2, 128), BF16) for i in range(2)]
            psF = [bank(6), bank(7)]

            for h in range(2):
                self.dma(w1[:, h * 64:(h + 1) * 64], W["filt_w1"][0], r=[], w=["fw1"])
                self.dma(w2[0:64, h * 64:(h + 1) * 64], W["filt_w2"][0], r=[], w=["fw2"])
                for i, nm in enumerate(("filt_freq", "filt_b1", "filt_b2")):
                    self.dma(fp[h * 64:(h + 1) * 64, i:i + 1], W[nm][0].rearrange("(p o) -> p o", o=1), r=[], w=["fpar"], slow=True)
                for c0 in (0, 1024, 2048, 3072):
                    s = self.stg_i % 2
                    self.stg_i += 1
                    self.dma(self.stg[s][h * 64:(h + 1) * 64, :], W["filt_w3"][0][:, c0:c0 + 1024], r=[], w=[("stg", s)])
                    self.copy("pool", w3r[h * 64:(h + 1) * 64, c0:c0 + 1024], self.stg[s][h * 64:(h + 1) * 64, :], r=[("stg", s)], w=["w3r"])
            w3v = w3r[:].rearrange("p (d o s c) -> p d o s c", d=2, o=2, s=NSC)
            for dd in range(2):
                for o in range(2):
                    self.copy("pool", w3s[:, dd, :, o, :], w3v[:, dd, o, :, :], r=["w3r"], w=["w3s"])
            self.tt("dve", fp[:, 3:4], fp[:, 0:1], fp[:, 1:2], ALU.mult, r=["fpar"], w=["fpar2"])
            self.tt("dve", fp[:, 4:5], fp[:, 0:1], fp[:, 2:3], ALU.mult, r=["fpar"], w=["fpar2"])
            self.dma(delta[:], self.din["c_delta"].partition_broadcast(128), r=[], w=["delta"])
            self.dma(tneg[:].rearrange("p a b -> p (a b)"), d["tneg"], r=[], w=["tneg"])
            self.ts("pool", w3s[:, 1, :, :, :], w3s[:, 1, :, :, :], -1.0, None, ALU.mult, None, r=["w3s"], w=["w3s"])

            def sin_layer(psrc, fbcol, dst, lo, rk, wk):
                self.ts("dve", hA[:], psrc, fp[:, 0:1], fp[:, fbcol:fbcol + 1], ALU.mult, ALU.add, r=[rk, "fpar", "fpar2"], w=["hA"])
                self.ts("dve", hB[:], hA[:], 1.0 / TWO_PI, MAGIC, ALU.mult, ALU.add, r=["hA"], w=["hB"])
                self.ts("dve", hB[:], hB[:], MAGIC, -TWO_PI, ALU.subtract, ALU.mult, r=["hB"], w=["hB"])
                self.tt("dve", hA[:], hA[:], hB[:], ALU.add, r=["hA", "hB"], w=["hA"])
                n_ = dst.shape[0]
                self.act(dst, hA[lo:lo + n_, :], AF.Sin, r=["hA"], w=[wk], scale=0.999999)

            nblk = 2 * NP // 512
            for b in range(nblk):
                zb = b % 2
                half = (b * 512) // NP
                c0 = b * 512 - half * NP
                self.dma(zst[zb][:], d["zemb"][:, b * 512:(b + 1) * 512], r=[], w=[("zst", zb)])
                self.mm(psF[0], w1[:, :], zst[zb][:], True, True, r=["fw1", ("zst", zb)], w=[("pb", 6)])
                sin_layer(psF[0], 3, h1[:], 0, ("pb", 6), "h1")
                self.mm(psF[1], w2[0:64, :], h1[0:64, :], True, True, r=["fw2", "h1"], w=[("pb", 7)])
                lo = half * 64
                sin_layer(psF[1], 4, h2T[lo:lo + 64, c0:c0 + 512], lo, ("pb", 7), "h2T")
            self.memset("pool", h2T[64:128, 0:1], 0.0, w=["h2T"])

            step_ctr = [0]

            def ktgen_steps(sc):
                kb = sc % 2
                ch0 = sc * Jc
                steps = []
                for half in range(2):
                    hp = slice(64 * half, 64 * half + 64)
                    for nb in range(N1 // nb1):
                        c = step_ctr[0]
                        step_ctr[0] += 1
                        pf = 4 + c % 4
                        t2 = c % 2

                        def pre(half=half, hp=hp, nb=nb, pf=pf, t2=t2):
                            psv = bank(pf).rearrange("p (n o c) -> p n o c", n=nb1, o=2)
                            for i in range(nb1):
                                n1 = nb * nb1 + i
                                self.mm(psv[:, i, :, :], h2T[hp, n1 * 128:(n1 + 1) * 128], w3s[hp, half, sc, :, :], True, True,
                                        r=["h2T", "w3s"], w=[("pb", pf)])
                            self.tt("pool", argt[t2][:], tneg[:, half, nb * nb1:(nb + 1) * nb1].unsqueeze(2).to_broadcast([128, nb1, Jc]),
                                    delta[:, ch0:ch0 + Jc].unsqueeze(1).to_broadcast([128, nb1, Jc]), ALU.mult, r=["tneg", "delta"], w=[("argt", t2)])
                            self.act(e2t[t2][:], argt[t2][:], AF.Exp, r=[("argt", t2)], w=[("e2t", t2)])

                        def fin(half=half, nb=nb, pf=pf, t2=t2):
                            psv = bank(pf).rearrange("p (n o c) -> p n o c", n=nb1, o=2)
                            for o in range(2):
                                ktv = kt[kb][:, o, half, :].rearrange("p (c n) -> p c n", n=N1)[:, :, nb * nb1:(nb + 1) * nb1]
                                self.stt("dve", ktv, e2t[t2][:].rearrange("p n c -> p c n"), 0.05,
                                         psv[:, :, o, :].rearrange("p n c -> p c n"), ALU.add, ALU.mult,
                                         r=[("e2t", t2), ("pb", pf)], w=[("kt", kb)])
                        steps.append((pre, fin))
                return steps

            for (pre_, fin_) in ktgen_steps(0):
                pre_()
                fin_()
            ii = 0
            uc = 0
            for sc in range(NSC):
                kb = sc % 2
                nxt = ktgen_steps(sc + 1) if sc + 1 < NSC else []
                assert len(nxt) in (0, 16)
                items = []
                for o in range(2):
                    kr = uc % 2
                    uc += 1
                    for jp in range(8):
                        def x_of(u, ch2, kb=kb, o=o, jp=jp):
                            return kt[kb][:, o, ch2, (jp * 2 + u) * 128:(jp * 2 + u + 1) * 128]

                        st, b3, k3 = self.fwd_stages(T, Bf, bank, ii, x_of, 2, [("kt", kb)])
                        ii += 1
                        idx = o * 8 + jp
                        extra = []
                        if nxt:
                            if idx > 0:
                                extra.append(nxt[idx - 1][1])
                            extra.append(nxt[idx][0])
                        if extra:
                            f0 = st[0]

                            def F0x(f0=f0, extra=extra):
                                for ex in extra:
                                    ex()
                                f0()
                            st[0] = F0x

                        def F4c(b3=b3, k3=k3, kr=kr, jp=jp, sc=sc, o=o):
                            self.copy("act", kfo[kr][:, jp * 2:(jp + 1) * 2, :, :].rearrange("p u r k -> p r u k"),
                                      b3.rearrange("p (r u k) -> p r u k", r=2, u=2), r=[k3], w=[("kfo", kr)])
                            if jp == 7:
                                row = (sc * 2 + o) * 128
                                self.dma(d["kf"][row:row + 128, :], kfo[kr][:].rearrange("p a r k -> p (a r k)"), r=[("kfo", kr)], w=["kfscr"])
                        items.append(st + [F4c])
                self.skew_emit(items)
                if nxt:
                    nxt[15][1]()
            P.barrier()

    def phase_attn(self, G):
        P = self.P
        g, N1, L, NQX = G["g"], G["N1"], G["L"], G["NQX"]
        d = self.GI[g]
        W = self.W
        with ExitStack() as es:
            esp = ExitStack()
            ps = self.psum(es, "psA", (128, 4096), F32)
            bank = lambda i: ps[:, i * 512:(i + 1) * 512]
            wuq = self.sb(es, "wuq", (128, 3, 8, 256), BF16)
            TK = self.sb(es, "TK", (128, L), BF16)
            cqn = self.sb(es, "cqn", (128, 3, NQX), BF16)
            ropeq = self.sb(es, "ropeq", (128, NQX), F32)
            wkv = self.sb(esp, "wkv", (128, 8, 384), BF16)
            wqi = self.sb(esp, "wqi", (128, 8, 384), BF16)
            wukv = self.sb(esp, "wukv", (128, 2, 2048), BF16)
            gkv = self.sb(esp, "gkv", (128, 2), F32)
            gq = self.sb(esp, "gq", (128, 3), F32)
            ckt = [self.sb(esp, f"ckt{i}", (128, 2, 512), BF16) for i in range(2)]
            kst = [self.sb(esp, f"kst{i}", (128, 512), BF16) for i in range(2)]
            tkt = [self.sb(esp, f"tkt{i}", (128, 512), BF16) for i in range(2)]
            vst = [self.sb(esp, f"vst{i}", (128, 512), BF16) for i in range(2)]
            xs32 = [self.sb(esp, f"xs32_{i}", (128, 8, 512), F32) for i in range(2)]
            xsb = [self.sb(esp, f"xsb_{i}", (128, 8, 512), BF16) for i in range(2)]
            rk = [self.sb(esp, f"rk{i}", (128, 512), F32) for i in range(2)]
            sq = self.sb(esp, "sq", (128, 3, 512), BF16)
            rstd = self.sb(esp, "rstd", (128, 512), F32)

            w_in = W["w_in"][0]
            self.load_w(wkv, w_in[:, IN_KV0:IN_KV0 + 320], 8, 320, "wkv")
            self.load_w(wqi, w_in[:, 0:384], 8, 384, "wqi")
            self.ts("pool", wkv[:, :, 320:352], wkv[:, :, 288:320], -1.0, None, ALU.mult, None, r=["wkv"], w=["wkv"])
            self.copy("pool", wkv[:, :, 352:384], wkv[:, :, 256:288], r=["wkv"], w=["wkv"])
            for m in range(3):
                for h in range(NH):
                    s = self.stg_i % 2
                    self.stg_i += 1
                    self.dma(self.stg[s][:, 0:192], W["w_uq"][0][m * 128:(m + 1) * 128, h * 192:(h + 1) * 192], r=[], w=[("stg", s)])
                    self.copy("pool", wuq[:, m, h, 0:192], self.stg[s][:, 0:192], r=[("stg", s)], w=["wuq"])
            self.ts("pool", wuq[:, :, :, 192:224], wuq[:, :, :, 160:192], -1.0, None, ALU.mult, None, r=["wuq"], w=["wuq"])
            self.copy("pool", wuq[:, :, :, 224:256], wuq[:, :, :, 128:160], r=["wuq"], w=["wuq"])
            self.load_w(wukv, W["w_ukv"][0], 2, 2048, "wukv")
            self.dma(gkv[:], W["kv_norm_g"][0].rearrange("(a p) -> p a", p=128), r=[], w=["gkv"], slow=True)
            self.dma(gq[:], W["q_norm_g"][0].rearrange("(a p) -> p a", p=128), r=[], w=["gq"], slow=True)
            self.dma(ropeq[:], d["ropeq"], r=[], w=["ropeq"])

            xv = d["xtp"].rearrange("(kc kp) c -> kp kc c", kp=128)
            xbv = d["xtb"].rearrange("(kc kp) c -> kp kc c", kp=128)
            chunks = [(0, 128)] + [(128 + i * 512, 128 + (i + 1) * 512) for i in range(L // 512)] + [(128 + L, 256 + L)]

            def rms_to(psums, nchunk, n, gcol, dst_of, inv_dim, pss, keyset, dkey):
                for m in range(nchunk):
                    self.act(sq[:, m, 0:n], psums[m], AF.Square, r=[keyset[m]], w=["sq"])
                for m in range(nchunk):
                    self.mm(pss[:, 0:n], self.ones[:], sq[:, m, 0:n], m == 0, m == nchunk - 1, r=["sq", "ones"], w=[keyset[-1]])
                self.act(rstd[:, 0:n], pss[:, 0:n], AF.Sqrt, r=[keyset[-1], "epsc"], w=["rstd"], bias=self.epsc[:, 1:2], scale=inv_dim)
                P.op("dve", lambda e: e.reciprocal(out=rstd[:, 0:n], in_=rstd[:, 0:n]), r=["rstd"], w=["rstd"])
                for m in range(nchunk):
                    self.stt("dve", dst_of(m), psums[m], gcol[:, m:m + 1], rstd[:, 0:n], ALU.mult, ALU.mult,
                             r=[keyset[m], "rstd", "gkv", "gq"], w=[dkey])

            for ci, (c0, c1) in enumerate(chunks):
                n = c1 - c0
                xb = ci % 2
                self.dma(xs32[xb][:, :, 0:n], xv[:, :, c0:c1], r=[], w=[("xs32", xb)])
                self.copy("pool" if ci % 2 else "act", xsb[xb][:, :, 0:n], xs32[xb][:, :, 0:n], r=[("xs32", xb)], w=[("xsb", xb)])
                self.dma(xbv[:, :, c0:c1], xsb[xb][:, :, 0:n], r=[("xsb", xb)], w=["xtbscr"])
                if 128 <= c0 < 128 + L:
                    k0 = c0 - 128
                    cb = ci % 2
                    pk = [("psk", i) for i in range(4)]
                    for mi, (cs, ce) in enumerate([(0, 128), (128, 256), (256, 384)]):
                        for kc in range(8):
                            self.mm(bank(mi), wkv[:, kc, cs:ce], xsb[xb][:, kc, 0:512], kc == 0, kc == 7, r=["wkv", ("xsb", xb)], w=[pk[mi]])
                    self.dma(rk[xb][:], d["ropek"][:, k0:k0 + 512], r=[], w=[("rk", xb)])
                    rms_to([bank(0), bank(1)], 2, 512, gkv, lambda m: ckt[cb][:, m, :], 1.0 / KVL, bank(3), [pk[0], pk[1], pk[3]], ("ckt", cb))
                    self.tt("dve", tkt[cb][:], bank(2), rk[xb][:], ALU.mult, r=[pk[2], ("rk", xb)], w=[("tkt", cb)])
                    self.mm(bank(2), self.stack2[:], tkt[cb][:], True, True, r=["stack2", ("tkt", cb)], w=[pk[2]])
                    self.copy("act", TK[:, k0:k0 + 512], bank(2), r=[pk[2]], w=["TK"])
                    for h in range(NH):
                        pbk = 4 + 2 * (h % 2)
                        sk = (ci * NH + h) % 2
                        for a in range(2):
                            self.mm(bank(pbk), wukv[:, a, h * 256:h * 256 + 128], ckt[cb][:, a, :], a == 0, a == 1,
                                    r=["wukv", ("ckt", cb)], w=[("psk", pbk)])
                        self.copy("act", kst[sk][:], bank(pbk), r=[("psk", pbk)], w=[("kst", sk)])
                        self.dma(d["kts"][h * 128:(h + 1) * 128, k0:k0 + 512], kst[sk][:], r=[("kst", sk)], w=["ktscr"])
                        for u in range(4):
                            for a in range(2):
                                self.mm(bank(pbk + 1)[:, u * 128:(u + 1) * 128], ckt[cb][:, a, u * 128:(u + 1) * 128],
                                        wukv[:, a, h * 256 + 128:h * 256 + 256], a == 0, a == 1, r=["wukv", ("ckt", cb)], w=[("psk", pbk + 1)])
                        self.copy("dve" if h % 2 else "act", vst[sk][:], bank(pbk + 1), r=[("psk", pbk + 1)], w=[("vst", sk)])
                        self.dma(d["vts"][h * 128:(h + 1) * 128, k0:k0 + 512], vst[sk][:], r=[("vst", sk)], w=["vtscr"])

            xqv = d["xqT"].rearrange("(kc kp) c -> kp kc c", kp=128)
            for qi, q0 in enumerate(range(0, NQX, 512)):
                n = min(512, NQX - q0)
                xb = qi % 2
                self.dma(xs32[xb][:, :, 0:n], xqv[:, :, q0:q0 + n], r=[], w=[("xs32", xb)])
                self.copy("pool", xsb[xb][:, :, 0:n], xs32[xb][:, :, 0:n], r=[("xs32", xb)], w=[("xsb", xb)])
                bs = 4 * (qi % 2)
                pk = [("psk", bs + i) for i in range(4)]
                for mi in range(3):
                    for kc in range(8):
                        self.mm(bank(bs + mi)[:, 0:n], wqi[:, kc, mi * 128:(mi + 1) * 128], xsb[xb][:, kc, 0:n], kc == 0, kc == 7,
                                r=["wqi", ("xsb", xb)], w=[pk[mi]])
                rms_to([bank(bs + i)[:, 0:n] for i in range(3)], 3, n, gq, lambda m: cqn[:, m, q0:q0 + n], 1.0 / QL, bank(bs + 3), pk, "cqn")
            P.barrier()
            esp.close()

            with ExitStack() as es2:
                KT = self.sb(es2, "KT", (128, L), BF16)
                V = self.sb(es2, "V", (128, L // 128, 128), BF16)
                QN = self.sb(es2, "QN", (128, NQX), BF16)
                QR = self.sb(es2, "QR", (128, NQX), BF16)
                tq = self.sb(es2, "tq", (128, 512), BF16)
                PT = [self.sb(es2, f"PT{i}", (128, 1024), BF16) for i in range(2)]
                acc = self.sb(es2, "acc", (128, 1024), F32)
                accb = self.sb(es2, "accb", (128, 1024), BF16)
                rs = self.sb(es2, "rs", (128, 1024), F32)
                osb = [self.sb(es2, f"osb{i}", (128, 1024), BF16) for i in range(2)]
                Sb = [ps[:, 0:1024], ps[:, 1024:2048]]
                Ob = ps[:, 2048:3072]
                m6, m7 = bank(6), bank(7)
                nkt = L // 128
                oi = 0
                for h in range(NH):
                    self.dma(KT[:], d["kts"][h * 128:(h + 1) * 128, :], r=["ktscr"], w=["KT"])
                    self.dma(V[:].rearrange("p t d -> p (t d)"), d["vts"][h * 128:(h + 1) * 128, :], r=["vtscr"], w=["V"])
                    for qi, q0 in enumerate(range(0, NQX, 512)):
                        n = min(512, NQX - q0)
                        for m in range(3):
                            self.mm(m6[:, 0:n], wuq[:, m, h, 0:128], cqn[:, m, q0:q0 + n], m == 0, m == 2, r=["wuq", "cqn"], w=[("pm", 6)])
                        self.copy("act", QN[:, q0:q0 + n], m6[:, 0:n], r=[("pm", 6)], w=["QN"])
                        for m in range(3):
                            self.mm(m7[:, 0:n], wuq[:, m, h, 128:256], cqn[:, m, q0:q0 + n], m == 0, m == 2, r=["wuq", "cqn"], w=[("pm", 7)])
                        self.tt("dve", tq[:, 0:n], m7[:, 0:n], ropeq[:, q0:q0 + n], ALU.mult, r=[("pm", 7), "ropeq"], w=["tq"])
                        self.mm(m6[:, 0:n], self.stack2[:], tq[:, 0:n], True, True, r=["stack2", "tq"], w=[("pm", 6)])
                        self.copy("act", QR[:, q0:q0 + n], m6[:, 0:n], r=[("pm", 6)], w=["QR"])
                    for q0 in range(0, NQX, 1024):
                        nq = min(1024, NQX - q0)
                        subs = [(s0, min(512, nq - s0)) for s0 in range(0, nq, 512)]
                        self.memset("pool", acc[:, 0:nq], 0.0, w=["acc"])

                        def qk(kt_):
                            sbk = kt_ % 2
                            for (s0, sn) in subs:
                                o = Sb[sbk][:, s0:s0 + sn]
                                self.mm(o, KT[:, kt_ * 128:(kt_ + 1) * 128], QN[:, q0 + s0:q0 + s0 + sn], True, False, r=["KT", "QN"], w=[("S", sbk)])
                            for si, (s0, sn) in enumerate(subs):
                                o = Sb[sbk][:, s0:s0 + sn]
                                hp_ = slice(64 * (si % 2), 64 * (si % 2) + 64)
                                self.mm(o, TK[hp_, kt_ * 128:(kt_ + 1) * 128], QR[hp_, q0 + s0:q0 + s0 + sn], False, True, r=["TK", "QR"], w=[("S", sbk)])

                        qk(0)
                        for kt_ in range(nkt):
                            sbk = kt_ % 2
                            if kt_ + 1 < nkt:
                                qk(kt_ + 1)
                            self.act(PT[sbk][:, 0:nq], Sb[sbk][:, 0:nq], AF.Exp, r=[("S", sbk)], w=[("PT", sbk)], scale=ATT_SCALE)
                            self.tt("dve", acc[:, 0:nq], acc[:, 0:nq], PT[sbk][:, 0:nq], ALU.add, r=["acc", ("PT", sbk)], w=["acc"])
                            for (s0, sn) in subs:
                                self.mm(Ob[:, s0:s0 + sn], V[:, kt_, :], PT[sbk][:, s0:s0 + sn], kt_ == 0, kt_ == nkt - 1, r=["V", ("PT", sbk)], w=["O"])
                        self.copy("pool", accb[:, 0:nq], acc[:, 0:nq], r=["acc"], w=["accb"])
                        for (s0, sn) in subs:
                            self.mm(Sb[0][:, s0:s0 + sn], self.ones[:], accb[:, s0:s0 + sn], True, True, r=["ones", "accb"], w=[("S", 0)])
                        P.op("dve", lambda e, nq=nq: e.reciprocal(out=rs[:, 0:nq], in_=Sb[0][:, 0:nq]), r=[("S", 0)], w=["rs"])
                        ob = oi % 2
                        oi += 1
                        self.tt("dve", osb[ob][:, 0:nq], Ob[:, 0:nq], rs[:, 0:nq], ALU.mult, r=["O", "rs"], w=[("osb", ob)])
                        self.dma(d["ot"][h * 128:(h + 1) * 128, q0:q0 + nq], osb[ob][:, 0:nq], r=[("osb", ob)], w=["otscr"])
            P.barrier()

    def phase_hyena(self, G):
        P = self.P
        g, N1, L, NQX, JI, Jc, NB, NSEL = G["g"], G["N1"], G["L"], G["NQX"], G["JI"], G["Jc"], G["NB"], G["NSEL"]
        d = self.GI[g]
        W = self.W
        w_in = W["w_in"][0]
        nsub = JI // Jc
        with ExitStack() as es:
            T = self.load_fft_tables(es, G)
            wstat = (3 * JI <= 128)
            if wstat:
                ps = self.psum(es, "psH", (128, 3584), F32)
                pst = self.psum(es, "psHT", (128, 1024), BF16)
            else:
                ps = self.psum(es, "psH", (128, 4096), F32)
            bank = lambda i: ps[:, i * 512:(i + 1) * 512]
            Bf = self.fft_bufs(es, inverse=True)
            kfb = [self.sb(es, f"kfb{i}", (128, 16, 2, 128), BF16) for i in range(2)]
            whc = [self.sb(es, f"whc{i}", (128, 8, 3 * JI), BF16) for i in range(2)]
            taps = [self.sb(es, f"taps{i}", (128, 3, 3, JI), F32) for i in range(2)]
            tbias = [self.sb(es, f"tbias{i}", (128, 3, JI), F32) for i in range(2)]
            skp = [self.sb(es, f"skp{i}", (128, 2, JI), F32) for i in range(2)]
            uraw = [self.sb(es, f"uraw{i}", (128, 3 * JI, NB), BF16) for i in range(2)]
            early = (N1 == 128)
            if early:
                xcA = [[self.sb(es, f"xcA{r_}_{i}", (128, Jc * N1), BF16) for i in range(3)] for r_ in range(2)]
                xcB = [self.sb(es, f"xcB_{i}", (128, Jc * N1), BF16) for i in range(3)]
                xc = None
            else:
                xc = [[self.sb(es, f"xc{r_}_{i}", (128, Jc * N1), BF16) for i in range(3)] for r_ in range(2)]

            def xcbuf(cj, sub):
                if early:
                    if sub == 0:
                        return xcA[cj % 2], (lambda t, cj=cj: ("xcA", cj % 2, t))
                    return xcB, (lambda t: ("xcB", t))
                return xc[sub % 2], (lambda t, sub=sub: ("xc", sub % 2, t))
            z1 = [self.sb(es, f"z1_{r_}", (128, Jc * N1), BF16) for r_ in range(2)]
            z2 = self.sb(es, "z2", (128, JI * N1), BF16)
            nhh = 2 if Jc <= 16 else 4
            Jh = Jc // nhh
            cacc = self.sb(es, "cacc", (128, Jh, N1), F32)
            ctmp = self.sb(es, "ctmp", (128, Jh, N1), F32)
            XW = 512 if N1 == 128 else 128
            xbl = [self.sb(es, f"xbl{i}", (128, 8, XW), BF16) for i in range(2)]
            uT = [self.sb(es, f"uT{i}", (128, 512), BF16) for i in range(2)] if 3 * JI <= 128 else None
            whv = self.whyb.rearrange("(kc kp) c -> kp kc c", kp=128)
            sel = self.sb(es, "sel", (128, NSEL), BF16)
            MZ = min(128, JI)
            ztc = self.sb(es, "ztc", (MZ, NSEL, N1), BF16)
            self.dma(self.stg[0][:, 0:NSEL], d["sel"], r=[], w=[("stg", 0)])
            self.copy("pool", sel[:], self.stg[0][:, 0:NSEL], r=[("stg", 0)], w=["sel"])
            xbv = d["xtb"].rearrange("(kc kp) c -> kp kc c", kp=128)
            ncolb = NB * 128
            ii = 0
            nchan = 256 // N1
            wstat_dummy = False
            assert nsub == 2 and 3 * JI <= 512
            ld_ctr = [0]

            def inproj_steps(cj):
                rb = cj % 2
                cc0 = cj * JI
                steps = []

                def params():
                    for t in range(3):
                        self.dma(whc[rb][:, :, t * JI:(t + 1) * JI], whv[:, :, t * 1024 + cc0:t * 1024 + cc0 + JI], r=["wscr"], w=[("whc", rb)])
                        for tap in range(3):
                            self.dma(taps[rb][:, tap, t, :], W["short_w"][0][tap:tap + 1, t * 1024 + cc0:t * 1024 + cc0 + JI].partition_broadcast(128),
                                     r=[], w=[("taps", rb)])
                        self.dma(tbias[rb][:, t, :], W["short_b"][:, t * 1024 + cc0:t * 1024 + cc0 + JI].partition_broadcast(128), r=[], w=[("taps", rb)])
                    for o in range(2):
                        self.dma(skp[rb][:, o, :], W["hy_skip"][0][o:o + 1, cc0:cc0 + JI].partition_broadcast(128), r=[], w=[("skp", rb)])
                steps.append(params)
                for b0 in range(0, ncolb, XW):
                    def step(b0=b0):
                        nn = min(XW, ncolb - b0)
                        xb = ld_ctr[0] % 2
                        ld_ctr[0] += 1
                        self.dma(xbl[xb][:, :, 0:nn], xbv[:, :, b0:b0 + nn], r=["xtbscr"], w=[("xbl", xb)])
                        if wstat:
                            M3 = 3 * JI
                            nbk = nn // 128
                            blk0 = b0 // 128
                            for kc in range(8):
                                self.mm(bank(1)[0:M3, 0:nn], whc[rb][:, kc, :], xbl[xb][:, kc, 0:nn], kc == 0, kc == 7,
                                        r=[("xbl", xb), ("whc", rb)], w=[("pb", 1)])
                            self.copy("act", uT[xb][0:M3, 0:nn], bank(1)[0:M3, 0:nn], r=[("pb", 1)], w=[("uT", xb)])
                            for u in range(nbk):
                                P.op("pe", lambda e, u=u, xb=xb: e.transpose(pst[:, u * M3:(u + 1) * M3], uT[xb][0:M3, u * 128:(u + 1) * 128],
                                                                              self.identb[0:M3, 0:M3]),
                                     r=[("uT", xb), "identb"], w=[("pb", 7)])
                            self.copy("act", uraw[rb][:, :, blk0:blk0 + nbk], pst[:, 0:nbk * M3].rearrange("p (u c) -> p c u", u=nbk),
                                      r=[("pb", 7)], w=[("uraw", rb)])
                        else:
                            for u in range(nn // 128):
                                blk = b0 // 128 + u
                                pbi = 1 if blk % 2 else 7
                                for kc in range(8):
                                    self.mm(bank(pbi)[:, 0:3 * JI], xbl[xb][:, kc, u * 128:(u + 1) * 128], whc[rb][:, kc, :], kc == 0, kc == 7,
                                            r=[("xbl", xb), ("whc", rb)], w=[("pb", pbi)])
                                self.copy("act", uraw[rb][:, :, blk], bank(pbi)[:, 0:3 * JI], r=[("pb", pbi)], w=[("uraw", rb)])
                    steps.append(step)
                return steps

            for st_ in inproj_steps(0):
                st_()
            for ci in range(G["NCI"]):
                c0 = ci * JI
                rb = ci % 2
                nxt = inproj_steps(ci + 1) if ci + 1 < G["NCI"] else []

                def conv3_piece(cj, sub, t, hh):
                    s0 = sub * Jc
                    rbj = cj % 2
                    xct, xkey = xcbuf(cj, sub)
                    ca = t * JI + s0 + hh * Jh
                    src = lambda lo: uraw[rbj][:, ca:ca + Jh, lo:lo + N1]
                    tp = lambda tap: taps[rbj][:, tap, t, s0 + hh * Jh:s0 + (hh + 1) * Jh].unsqueeze(2).to_broadcast([128, Jh, N1])
                    self.tt("dve", cacc[:], src(0), tp(0), ALU.mult, r=[("uraw", rbj), ("taps", rbj)], w=["cacc"])
                    self.tt("pool", ctmp[:], src(1), tp(1), ALU.mult, r=[("uraw", rbj), ("taps", rbj)], w=["ctmp"])
                    self.tt("dve", cacc[:], cacc[:], ctmp[:], ALU.add, r=["cacc", "ctmp"], w=["cacc"])
                    self.tt("pool", ctmp[:], src(2), tp(2), ALU.mult, r=[("uraw", rbj), ("taps", rbj)], w=["ctmp"])
                    self.tt("dve", cacc[:], cacc[:], ctmp[:], ALU.add, r=["cacc", "ctmp"], w=["cacc"])
                    self.tt("dve", xct[t][:, hh * Jh * N1:(hh + 1) * Jh * N1].rearrange("p (c n) -> p c n", n=N1), cacc[:],
                            tbias[rbj][:, t, s0 + hh * Jh:s0 + (hh + 1) * Jh].unsqueeze(2).to_broadcast([128, Jh, N1]), ALU.add,
                            r=["cacc", ("taps", rbj)], w=[xkey(t)])

                def conv3(cj, sub):
                    for t in range(3):
                        for hh in range(nhh):
                            conv3_piece(cj, sub, t, hh)

                if early and ci == 0:
                    conv3(0, 0)

                s1_pieces = [(t, hh) for t in range(3) for hh in range(nhh)]

                def kf_load(sc, o, kb):
                    row = (sc * 2 + o) * 128
                    self.dma(kfb[kb][:].rearrange("p a r k -> p (a r k)"), d["kf"][row:row + 128, :], r=["kfscr"], w=[("kfb", kb)])

                flat = [(0, 0), (1, 0), (0, 1), (1, 1)]
                items = []
                kf_load(ci * nsub + flat[0][0], flat[0][1], 0)
                nit = 0
                for ui, (sub, o) in enumerate(flat):
                    kb = ui % 2
                    rr = sub % 2
                    sc = ci * nsub + sub
                    xct_, xkey_ = xcbuf(ci, sub)
                    cur = xct_[2] if o == 0 else z1[rr]
                    cur_key = xkey_(2) if o == 0 else ("z1", rr)
                    gate = xct_[o]
                    gate_key = xkey_(o)
                    for jp in range(8):
                        cols = slice(jp * 256, (jp + 1) * 256)

                        def x_of(u, ch2, cur=cur, jp=jp):
                            return cur[:, (jp * 2 + u) * 128:(jp * 2 + u + 1) * 128]

                        st, b3, k3 = self.fwd_stages(T, Bf, bank, ii, x_of, 1, [cur_key])
                        ch_lo = sub * Jc + jp * nchan
                        skb = skp[rb][:, o, ch_lo:ch_lo + nchan].unsqueeze(2).to_broadcast([128, nchan, N1])
                        if o == 0:
                            dst, dkey = z1[rr][:, cols], ("z1", rr)
                        else:
                            dst, dkey = z2[:, sub * Jc * N1 + jp * 256:sub * Jc * N1 + (jp + 1) * 256], "z2"
                        st2 = self.inv_stages(T, Bf, bank, ii, b3, k3, kfb[kb][:, jp * 2:(jp + 1) * 2, :, :], ("kfb", kb), cur[:, cols], cur_key,
                                              gate[:, cols], gate_key, skb, dst, dkey, N1, ("skp", rb))
                        ii += 1
                        pre = []
                        if jp == 0 and o == 0 and sub == 0 and not early:
                            pre.append(lambda ci=ci: conv3(ci, 0))
                        npc = (len(s1_pieces) + 7) // 8
                        if o == 0 and sub == 0:
                            for (t_, hh_) in s1_pieces[jp * npc:(jp + 1) * npc]:
                                pre.append(lambda t_=t_, hh_=hh_, ci=ci: conv3_piece(ci, 1, t_, hh_))
                        if early and o == 1 and sub == 1 and ci + 1 < G["NCI"]:
                            for (t_, hh_) in s1_pieces[jp * npc:(jp + 1) * npc]:
                                pre.append(lambda t_=t_, hh_=hh_, ci=ci: conv3_piece(ci + 1, 0, t_, hh_))
                        if jp == 5 and ui + 1 < len(flat):
                            nsub_, no_ = flat[ui + 1]
                            pre.append(lambda nsub_=nsub_, no_=no_, kb2=(ui + 1) % 2: kf_load(ci * nsub + nsub_, no_, kb2))
                        nspread = 24 if early else 32
                        if nit < nspread:
                            lo_, hi_ = nit * len(nxt) // nspread, (nit + 1) * len(nxt) // nspread
                            pre.extend(nxt[lo_:hi_])
                        nit += 1
                        if pre:
                            f0 = st[0]

                            def F0x(f0=f0, pre=pre):
                                for p_ in pre:
                                    p_()
                                f0()
                            st[0] = F0x
                        items.append(st + st2)
                self.skew_emit(items)
                for c1 in range(0, JI, 128):
                    mz = min(128, JI - c1)
                    z2v = z2[:, c1 * N1:(c1 + mz) * N1].rearrange("p (c n) -> p n c", n=N1)
                    per = 512 // NSEL
                    for nb0 in range(0, N1, per):
                        nn = min(per, N1 - nb0)
                        pb = 4 + (nb0 // per) % 2
                        for i in range(nn):
                            self.mm(bank(pb)[0:mz, i * NSEL:(i + 1) * NSEL], z2v[:, nb0 + i, :], sel[:], True, True, r=["z2", "sel"], w=[("pb", pb)])
                        self.copy("act", ztc[0:mz, :, nb0:nb0 + nn].rearrange("p j n -> p n j"),
                                  bank(pb)[0:mz, 0:nn * NSEL].rearrange("p (n j) -> p n j", j=NSEL), r=[("pb", pb)], w=["ztc"])
                    flat_z = ztc[0:mz, :, :].rearrange("p j n -> p (j n)")
                    self.dma(d["zt"][c0 + c1:c0 + c1 + mz, :], flat_z[:, N1 - 1:N1 - 1 + NQX], r=["ztc"], w=["ztscr"])
            P.barrier()

    def phase_c1(self, G):
        P = self.P
        g, NQX, NQ = G["g"], G["NQX"], G["NQ"]
        d = self.GI[g]
        W = self.W
        w_in = W["w_in"][0]
        with ExitStack() as es:
            y1tv = d["y1t"].rearrange("(kc kp) c -> kp kc c", kp=128)
            lng = self.sb(es, "lng", (128, 1024), F32)
            lnb = self.sb(es, "lnb", (128, 1024), F32)
            xres = [self.sb(es, f"xres{i}", (128, 1024), F32) for i in range(2)]
            pre = self.sb(es, "pre", (128, 1024), F32)
            yo = [self.sb(es, f"yo{i}", (128, 1024), F32) for i in range(2)]
            with ExitStack() as es1:
                ps = self.psum(es1, "psC1", (128, 3584), F32)
                pst = self.psum(es1, "psT", (128, 1024), BF16)
                bank = lambda i: ps[:, i * 512:(i + 1) * 512]
                wg = self.sb(es1, "wg", (128, 8, 2048), BF16)
                wohy = self.sb(es1, "wohy", (128, 8, 1024), BF16)
                womla = self.sb(es1, "womla", (128, 8, 1024), BF16)
                wout = self.sb(es1, "wout", (128, 8, 1024), BF16)
                xs32 = self.sb(es1, "c_xs32", (128, 8, 512), F32)
                xb = self.sb(es1, "c_xb", (128, 8, 512), BF16)
                ztb = self.sb(es1, "c_ztb", (128, 8, 512), BF16)
                otb = self.sb(es1, "c_otb", (128, 8, 512), BF16)
                sg = [self.sb(es1, f"c_sg{i}", (128, 512), F32) for i in range(2)]
                mt_ = [self.sb(es1, f"c_mt{i}", (128, 512), F32) for i in range(2)]
                merged = self.sb(es1, "merged", (128, 8, 512), BF16)
                y1b = self.sb(es1, "y1b", (128, 1024), BF16)
                y1Ts = [self.sb(es1, f"y1Ts{i}", (128, 8, 128), BF16) for i in range(2)]
                self.load_w(wg, w_in[:, IN_G0:IN_G0 + 2048], 8, 2048, "wg")
                self.load_w(wohy, W["w_o_hy"][0], 8, 1024, "wohy")
                self.load_w(womla, W["w_o_mla"][0], 8, 1024, "womla")
                self.load_w(wout, W["w_out"][0], 8, 1024, "wout")
                self.dma(lng[:], W["ln1_g"].partition_broadcast(128), r=[], w=["lnp"])
                self.dma(lnb[:], W["ln1_b"].partition_broadcast(128), r=[], w=["lnp"])
                xqv = d["xqT"].rearrange("(kc kp) c -> kp kc c", kp=128)
                ztv = d["zt"].rearrange("(kc kp) c -> kp kc c", kp=128)
                otv = d["ot"].rearrange("(kc kp) c -> kp kc c", kp=128)
                ti = 0
                for q0 in range(0, NQX, 512):
                    nb = min(512, NQX - q0)
                    self.dma(xs32[:, :, 0:nb], xqv[:, :, q0:q0 + nb], r=[], w=["c_xs32"])
                    self.copy("pool", xb[:, :, 0:nb], xs32[:, :, 0:nb], r=["c_xs32"], w=["c_xb"])
                    self.dma(ztb[:, :, 0:nb], ztv[:, :, q0:q0 + nb], r=["ztscr"], w=["c_ztb"])
                    self.dma(otb[:, :, 0:nb], otv[:, :, q0:q0 + nb], r=["otscr"], w=["c_otb"])
                    for m in range(8):
                        ms = slice(m * 128, (m + 1) * 128)
                        for kc in range(8):
                            self.mm(bank(0)[:, 0:nb], wg[:, kc, m * 128:(m + 1) * 128], xb[:, kc, 0:nb], kc == 0, kc == 7, r=["wg", "c_xb"], w=[("pb", 0)])
                        for kc in range(8):
                            self.mm(bank(1)[:, 0:nb], wohy[:, kc, ms], ztb[:, kc, 0:nb], kc == 0, kc == 7, r=["wohy", "c_ztb"], w=[("pb", 1)])
                        for kc in range(8):
                            self.mm(bank(2)[:, 0:nb], wg[:, kc, 1024 + m * 128:1024 + (m + 1) * 128], xb[:, kc, 0:nb], kc == 0, kc == 7, r=["wg", "c_xb"], w=[("pb", 2)])
                        for kc in range(8):
                            self.mm(bank(3)[:, 0:nb], womla[:, kc, ms], otb[:, kc, 0:nb], kc == 0, kc == 7, r=["womla", "c_otb"], w=[("pb", 3)])
                        self.act(sg[0][:, 0:nb], bank(0)[:, 0:nb], AF.Sigmoid, r=[("pb", 0)], w=[("sg", 0)])
                        self.act(sg[1][:, 0:nb], bank(2)[:, 0:nb], AF.Sigmoid, r=[("pb", 2)], w=[("sg", 1)])
                        self.tt("dve", mt_[0][:, 0:nb], bank(1)[:, 0:nb], sg[0][:, 0:nb], ALU.mult, r=[("pb", 1), ("sg", 0)], w=[("mt", 0)])
                        self.tt("dve", mt_[1][:, 0:nb], bank(3)[:, 0:nb], sg[1][:, 0:nb], ALU.mult, r=[("pb", 3), ("sg", 1)], w=[("mt", 1)])
                        self.tt("pool", merged[:, m, 0:nb], mt_[0][:, 0:nb], mt_[1][:, 0:nb], ALU.add, r=[("mt", 0), ("mt", 1)], w=["merged"])
                    for t0 in range(0, nb, 128):
                        n = min(128, nb - t0)
                        e0 = q0 + t0
                        xr = xres[ti % 2]
                        xk = ("xres", ti % 2)
                        ti += 1
                        self.dma(xr[0:n, :], d["xq"][e0:e0 + n, :], r=[], w=[xk])
                        for hf in range(2):
                            for m in range(8):
                                self.mm(bank(4 + hf)[0:n, :], merged[:, m, t0:t0 + n], wout[:, m, hf * 512:(hf + 1) * 512], m == 0, m == 7,
                                        r=["merged", "wout"], w=[("pb", 4 + hf)])
                        self.stt("dve", pre[0:n, :], xr[0:n, :], ALPHA, ps[0:n, 2048:3072], ALU.mult, ALU.add, r=[xk, ("pb", 4), ("pb", 5)], w=["pre"])
                        yt = yo[ti % 2]
                        yk = ("yo", ti % 2)
                        self.layer_norm(pre, yt, n, lng, lnb, "pre", yk, None)
                        self.dma(d["y1s"][e0:e0 + n, :], yt[0:n, :], r=[yk], w=["y1scr"])
                        self.copy("act", y1b[0:n, :], yt[0:n, :], r=[yk], w=["y1b"])
                        for c in range(8):
                            P.op("pe", lambda e, c=c, n=n: e.transpose(pst[:, c * 128:c * 128 + n], y1b[0:n, c * 128:(c + 1) * 128], self.identb[0:n, 0:n]),
                                 r=["y1b", "identb"], w=["pst"])
                        ys = ti % 2
                        self.copy("dve", y1Ts[ys][:, :, 0:n], pst[:].rearrange("p (c t) -> p c t", c=8)[:, :, 0:n], r=["pst"], w=[("y1Ts", ys)])
                        self.dma(y1tv[:, :, e0:e0 + n], y1Ts[ys][:, :, 0:n], r=[("y1Ts", ys)], w=["y1tscr"])
                P.barrier()

            with ExitStack() as es2:
                ps = self.psum(es2, "psC2", (128, 4096), F32)
                bank = lambda i: ps[:, i * 512:(i + 1) * 512]
                wdn = self.sb(es2, "wdn", (128, NF, 1024), BF16)
                wup = [self.sb(es2, f"wup{i}", (128, 8, 256), BF16) for i in range(2)]
                dww = self.sb(es2, "dww", (128, 3, NF), F32)
                dwb = self.sb(es2, "dwb", (128, NF), F32)
                hm = self.sb(es2, "hm", (128, 2), F32)
                aext = [self.sb(es2, f"aext{i}", (128, 514), F32) for i in range(2)]
                cv = self.sb(es2, "cv", (128, 512), F32)
                gl = self.sb(es2, "gl", (128, 512), F32)
                hmid = self.sb(es2, "hmid", (128, NF, 512), BF16)
                y1Tb = [self.sb(es2, f"y1Tb{i}", (128, 8, 514), BF16) for i in range(2)]
                self.dma(wdn[:], self.wdnb.rearrange("(f p) c -> p f c", p=128), r=["wscr"], w=["wdn"])
                for t in range(3):
                    self.dma(dww[:, t, :], W["dw_w"][0][t].rearrange("(f p) -> p f", p=128), r=[], w=["dww"], slow=True)
                self.dma(dwb[:], W["dw_b"][0].rearrange("(f p) -> p f", p=128), r=[], w=["dww"], slow=True)
                self.dma(hm[:], d["hm"], r=[], w=["hm"])
                self.dma(lng[:], W["ln2_g"].partition_broadcast(128), r=["lnp"], w=["lnp"])
                self.dma(lnb[:], W["ln2_b"].partition_broadcast(128), r=["lnp"], w=["lnp"])
                wupv = self.wupb.rearrange("(kc kp) c -> kp kc c", kp=128)
                nmt = NQ // 512
                wi = 0
                ti = 0
                for mt in range(nmt):
                    e0 = 0
                    y1T = y1Tb[mt % 2]
                    yTk = ("y1Tb", mt % 2)
                    self.dma(y1T[:], y1tv[:, :, 512 * mt:512 * mt + 514], r=["y1tscr"], w=[yTk])
                    for f in range(NF):
                        wb_ = wi % 2
                        wi += 1
                        self.dma(wup[wb_][:, :, 0:128], wupv[:, :, f * 128:(f + 1) * 128], r=["wscr"], w=[("wup", wb_)])
                        self.dma(wup[wb_][:, :, 128:256], wupv[:, :, DFF + f * 128:DFF + (f + 1) * 128], r=["wscr"], w=[("wup", wb_)])
                        pa = 2 * (f % 2)
                        ae = aext[f % 2]
                        ak = ("aext", f % 2)
                        for kc in range(8):
                            self.mm(bank(pa), wup[wb_][:, kc, 0:128], y1T[:, kc, e0:e0 + 512], kc == 0, kc == 7, r=[("wup", wb_), yTk], w=[("pb", pa)])
                        for kc in range(8):
                            self.mm(bank(pa + 1)[:, 0:2], wup[wb_][:, kc, 0:128], y1T[:, kc, e0 + 512:e0 + 514], kc == 0, kc == 7, r=[("wup", wb_), yTk], w=[("pb", pa + 1)])
                        pbk = 4 + f % 2
                        for kc in range(8):
                            self.mm(bank(pbk), wup[wb_][:, kc, 128:256], y1T[:, kc, e0 + 1:e0 + 513], kc == 0, kc == 7, r=[("wup", wb_), yTk], w=[("pb", pbk)])
                        self.copy("act", ae[:, 0:512], bank(pa), r=[("pb", pa)], w=[ak])
                        self.copy("act", ae[:, 512:514], bank(pa + 1)[:, 0:2], r=[("pb", pa + 1)], w=[ak])
                        if mt == 0:
                            self.ts("dve", ae[:, 0:1], ae[:, 0:1], hm[:, 0:1], None, ALU.mult, None, r=[ak, "hm"], w=[ak])
                        if mt == nmt - 1:
                            self.ts("dve", ae[:, 513:514], ae[:, 513:514], hm[:, 1:2], None, ALU.mult, None, r=[ak, "hm"], w=[ak])
                        self.ts("dve", cv[:], ae[:, 0:512], dww[:, 0, f:f + 1], None, ALU.mult, None, r=[ak, "dww"], w=["cv"])
                        self.stt("dve", cv[:], ae[:, 1:513], dww[:, 1, f:f + 1], cv[:], ALU.mult, ALU.add, r=[ak, "dww", "cv"], w=["cv"])
                        self.stt("dve", cv[:], ae[:, 2:514], dww[:, 2, f:f + 1], cv[:], ALU.mult, ALU.add, r=[ak, "dww", "cv"], w=["cv"])
                        self.act(gl[:], cv[:], AF.Gelu, r=["cv", "dww"], w=["gl"], bias=dwb[:, f:f + 1])
                        self.tt("dve", hmid[:, f, :], bank(pbk), gl[:], ALU.mult, r=[("pb", pbk), "gl"], w=["hmid"])
                    for tt_ in range(4):
                        tok0 = 512 * mt + tt_ * 128
                        xr = xres[ti % 2]
                        xk = ("xres", ti % 2)
                        ti += 1
                        self.dma(xr[:], d["y1s"][tok0 + 1:tok0 + 129, :], r=["y1scr"], w=[xk])
                        for hf in range(2):
                            for f in range(NF):
                                self.mm(bank(6 + hf), hmid[:, f, tt_ * 128:(tt_ + 1) * 128], wdn[:, f, hf * 512:(hf + 1) * 512], f == 0, f == NF - 1,
                                        r=["hmid", "wdn"], w=[("pb", 6 + hf)])
                        self.stt("dve", pre[:], xr[:], ALPHA, ps[:, 3072:4096], ALU.mult, ALU.add, r=[xk, ("pb", 6), ("pb", 7)], w=["pre"])
                        yt = yo[ti % 2]
                        yk = ("yo", ti % 2)
                        self.layer_norm(pre, yt, 128, lng, lnb, "pre", yk, None)
                        self.dma(d["y"][tok0:tok0 + 128, :], yt[:], r=[yk], w=["yout"])
                P.barrier()


_CACHE = {}


def _host_inputs(inputs, groups=(0, 1)):
    xs = [np.asarray(inputs["x_sample"], np.float32), np.asarray(inputs["x_prompt"], np.float32)]
    wnames = ["w_in", "short_w", "short_b", "q_norm_g", "w_uq", "kv_norm_g", "w_ukv", "w_o_mla", "filt_w1", "filt_b1", "filt_freq",
              "filt_w2", "filt_b2", "filt_w3", "hy_skip", "w_o_hy", "w_out", "ln1_g", "ln1_b", "w_ffn_up", "dw_w", "dw_b", "w_ffn_down",
              "ln2_g", "ln2_b"]
    base = {n: np.ascontiguousarray(np.asarray(inputs[n], np.float32)) for n in wnames}
    base["c_ident"] = np.eye(128, dtype=np.float32)
    st = np.zeros((128, 128), np.float32)
    for i in range(128):
        st[i, i % 64] = 1.0
        st[i, 64 + i % 64] = 1.0
    base["c_stack2"] = st
    gconst = {}
    for G in GROUPS:
        g, L, N1 = G["g"], G["L"], G["N1"]
        ft = _fft_tables(L, N1)
        zT, deltas, tneg = _filter_tables(L, N1)
        base["c_delta"] = deltas
        for k, v in ft.items():
            base[f"{k}{g}"] = v
        base[f"zemb{g}"] = zT
        base[f"tneg{g}"] = tneg
        tok = (np.arange(N1)[:, None] + N1 * np.arange(128)[None, :]).reshape(-1)
        base[f"ropek{g}"] = _rope_tables(tok)
    in_maps = []
    for core in range(8):
        m = dict(base)
        for G in GROUPS:
            g, L, N1, NQ, NQX, NB, NSEL = G["g"], G["L"], G["N1"], G["NQ"], G["NQX"], G["NB"], G["NSEL"]
            if g == 0:
                seq, t0 = core // 4, NQ * (core % 4)
            else:
                seq, t0 = core // 2, NQ * (core % 2)
            x = xs[g][seq]
            xpad = np.zeros((L + 2, 1024), np.float32)
            xpad[1:L + 1] = x
            idx = (np.arange(-1, N1 + 1)[:, None] + N1 * np.arange(128)[None, :]).reshape(-1) + 1
            m[f"xtp{g}"] = np.ascontiguousarray(xpad[idx].T)
            ext = np.arange(t0 - 1, t0 + NQ + 1)
            xq = xpad[ext + 1]
            m[f"xq{g}"] = np.ascontiguousarray(xq)
            m[f"xqT{g}"] = np.ascontiguousarray(xq.T)
            m[f"ropeq{g}"] = _rope_tables(np.clip(ext, 0, L - 1))
            sel = np.zeros((128, NSEL), np.float32)
            P0 = t0 // N1
            for j in range(NSEL):
                p = P0 - 1 + j
                if 0 <= p < 128:
                    sel[p, j] = 1.0
            m[f"sel{g}"] = sel
            hm = np.zeros((128, 2), np.float32)
            hm[:, 0] = 1.0 if t0 > 0 else 0.0
            hm[:, 1] = 1.0 if t0 + NQ < L else 0.0
            m[f"hm{g}"] = hm
        in_maps.append(m)
    return in_maps


def kernel(**inputs):
    if "nc" not in _CACHE:
        b = Builder()
        _CACHE["nc"] = b.build()
        _CACHE["b"] = b
    nc = _CACHE["nc"]
    in_maps = _host_inputs(inputs)
    names = set(_CACHE["b"].din.keys())
    in_maps = [{k: v for k, v in m.items() if k in names} for m in in_maps]
    res = run_bass_kernel_spmd(nc, in_maps, core_ids=list(range(8)))
    y_sample = np.zeros((2, 16384, 1024), np.float32)
    y_prompt = np.zeros((4, 4096, 1024), np.float32)
    for core in range(8):
        r = res.results[core]
        y_sample[core // 4, 4096 * (core % 4):4096 * (core % 4 + 1)] = r["y0"]
        y_prompt[core // 2, 2048 * (core % 2):2048 * (core % 2 + 1)] = r["y1"]
    return (y_prompt, y_sample)
```

```python
import math
from contextlib import ExitStack

import numpy as np
import concourse.bass as bass
import concourse.mybir as mybir
from concourse.bass_utils import run_bass_kernel_spmd

F32 = mybir.dt.float32
BF16 = mybir.dt.bfloat16
AF = mybir.ActivationFunctionType
ALU = mybir.AluOpType

ENGS = ("pe", "act", "dve", "pool", "sp")
NDMA_SEM = 24

D = 1024
NH = 8
QL = 384
KVL = 256
KR = 64
DFF = 2816
NF = DFF // 128
IN_KV0 = 384
IN_KR0 = 640
IN_HY0 = 704
IN_G0 = 3776
ALPHA = 2.0 ** 0.25
LN_EPS = 1e-5
RMS_EPS = 1e-6
ATT_SCALE = 192.0 ** -0.5
MAGIC = 12582912.0
TWO_PI = 2.0 * math.pi

GROUPS = [
    dict(g=0, L=16384, N1=128, NQ=4096, JI=32, Jc=16),
    dict(g=1, L=4096, N1=32, NQ=2048, JI=128, Jc=64),
]
for _G in GROUPS:
    _G["NQX"] = _G["NQ"] + 2
    _G["NSEL"] = _G["NQ"] // _G["N1"] + 2
    _G["NB"] = _G["N1"] + 2
    _G["cpk"] = 128 // _G["N1"]
    _G["NCI"] = 1024 // _G["JI"]
    _G["NSC"] = 1024 // _G["Jc"]
    _G["Jf"] = 2 * _G["Jc"]
    _G["NP"] = _G["N1"] * 128


class Op:
    __slots__ = ("eng", "fn", "r", "w", "dma", "waits", "dmawaits", "marked", "didx", "ie", "cnt")

    def __init__(self, eng, fn, r, w, dma):
        self.eng, self.fn, self.r, self.w, self.dma = eng, fn, tuple(r), tuple(w), dma
        self.waits = {}
        self.dmawaits = []
        self.marked = False
        self.didx = None
        self.ie = -1
        self.cnt = 0


class Prog:
    def __init__(self):
        self.ops = []

    def op(self, eng, fn, r=(), w=()):
        self.ops.append(Op(eng, fn, r, w, False))

    def dma(self, fn, r=(), w=(), eng="sp"):
        self.ops.append(Op(eng, fn, r, w, True))

    def barrier(self):
        self.ops.append(Op(None, None, (), (), False))

    def analyze(self):
        last_w, readers = {}, {}
        n_on = {e: 0 for e in ENGS}
        last_on = {e: None for e in ENGS}
        recent_dma = []
        bar = None
        ndma = 0
        for o in self.ops:
            if o.eng is None:
                bar = [dict(last_on), list(recent_dma[-NDMA_SEM:]), set(ENGS)]
                last_w, readers = {}, {}
                continue
            deps = []
            if bar is not None and o.eng in bar[2]:
                bar[2].discard(o.eng)
                deps.extend(p for p in bar[0].values() if p is not None)
                deps.extend(bar[1])
            for k in o.r:
                p = last_w.get(k)
                if p is not None:
                    deps.append(p)
            for k in o.w:
                p = last_w.get(k)
                if p is not None:
                    deps.append(p)
                deps.extend(readers.get(k, ()))
            for p in deps:
                if p is o:
                    continue
                if p.dma:
                    if p not in o.dmawaits:
                        o.dmawaits.append(p)
                else:
                    if p.eng == "pe" and o.eng == "pe" and not o.dma:
                        continue
                    cur = o.waits.get(p.eng)
                    if cur is None or p.ie > cur.ie:
                        o.waits[p.eng] = p
            for k in o.w:
                last_w[k] = o
                readers[k] = []
            for k in o.r:
                readers.setdefault(k, []).append(o)
            o.ie = n_on[o.eng]
            n_on[o.eng] += 1
            if not o.dma:
                last_on[o.eng] = o
            else:
                o.didx = ndma
                ndma += 1
                recent_dma.append(o)
        for o in self.ops:
            if o.eng is None:
                continue
            for p in o.waits.values():
                p.marked = True
        cnt = {e: 0 for e in ENGS}
        for o in self.ops:
            if o.eng is None or o.dma:
                continue
            if o.marked:
                cnt[o.eng] += 1
            o.cnt = cnt[o.eng]
        return cnt

    def emit(self, sems, dsems, block):
        streams = {e: [] for e in ENGS}
        known = {e: {f: 0 for f in ENGS} for e in ENGS}
        kd = {e: [0] * NDMA_SEM for e in ENGS}
        dma_ops = [o for o in self.ops if o.eng is not None and o.dma]
        for o in self.ops:
            if o.eng is None:
                continue
            e = o.eng
            st = streams[e]
            for f, p in o.waits.items():
                if p.cnt > known[e][f]:
                    known[e][f] = p.cnt
                    st.append(("w", sems[f], p.cnt))
            dws = list(o.dmawaits)
            if o.dma and o.didx >= NDMA_SEM:
                dws.append(dma_ops[o.didx - NDMA_SEM])
            for p in dws:
                slot, val = p.didx % NDMA_SEM, 16 * (p.didx // NDMA_SEM + 1)
                if val > kd[e][slot]:
                    kd[e][slot] = val
                    st.append(("w", dsems[slot], val))
            if o.dma:
                st.append(("i", o.fn, dsems[o.didx % NDMA_SEM], 16))
            elif o.marked:
                st.append(("i", o.fn, sems[e], 1))
            else:
                st.append(("i", o.fn, None, 0))
        for p in dma_ops[-NDMA_SEM:]:
            slot, val = p.didx % NDMA_SEM, 16 * (p.didx // NDMA_SEM + 1)
            if val > kd["sp"][slot]:
                kd["sp"][slot] = val
                streams["sp"].append(("w", dsems[slot], val))

        def run(eng_obj, st):
            for it in st:
                if it[0] == "w":
                    eng_obj.wait_ge(it[1], it[2])
                else:
                    ins = it[1](eng_obj)
                    if it[2] is not None:
                        ins.then_inc(it[2], it[3])

        @block.tensor
        def _(t):
            run(t, streams["pe"])

        @block.scalar
        def _(t):
            run(t, streams["act"])

        @block.vector
        def _(t):
            run(t, streams["dve"])

        @block.gpsimd
        def _(t):
            run(t, streams["pool"])

        @block.sync
        def _(t):
            run(t, streams["sp"])

        return {e: len(s) for e, s in streams.items()}


def _fft_tables(L, N1):
    N = 2 * L
    m = np.arange(128)
    kh = np.arange(128) + 0.5
    T = {}
    f2 = np.zeros((128, 2, 256))
    for ch in range(2):
        n2 = 128 * ch + m
        ang = TWO_PI * np.outer(n2, kh) / 256.0
        f2[:, ch, 0:128] = np.cos(ang)
        f2[:, ch, 128:256] = -np.sin(ang)
    T["f2"] = f2.reshape(128, 512)
    q = np.arange(128)
    n1 = q % N1
    c4 = q // N1
    ang = TWO_PI * np.outer(n1, kh) / N
    twr, twi = np.cos(ang), -np.sin(ang)
    T["twa"] = np.concatenate([twr, twr, twr, twr], axis=1)
    T["twb"] = np.concatenate([twi, twi, twi, twi], axis=1)
    ang = TWO_PI * np.outer(n1, n1) / N1
    same = (c4[:, None] == c4[None, :]).astype(np.float64)
    C = np.cos(ang) * same
    S = np.sin(ang) * same
    T["f1"] = np.concatenate([C, S, -S, -C], axis=1)
    T["r12"] = np.concatenate([C, S, -S, C, -C, -S], axis=1)
    ang = TWO_PI * np.outer(kh, n1) / N
    itr, iti = np.cos(ang), np.sin(ang)
    T["ita"] = np.concatenate([itr, itr, itr, itr], axis=1)
    T["itb"] = np.concatenate([iti, iti, iti, iti], axis=1)
    ang = TWO_PI * np.outer(kh, m) / 256.0
    T["ics"] = np.concatenate([np.cos(ang) * (2.0 / N), -np.sin(ang) * (2.0 / N), -np.cos(ang) * (2.0 / N)], axis=1)
    return {k: np.ascontiguousarray(v, dtype=np.float32) for k, v in T.items()}


def _filter_tables(L, N1):
    N = 2 * L
    NP = N1 * 128
    n1 = np.arange(N1)
    m = np.arange(128)
    pos = np.zeros((2, N1, 128))
    for ch in range(2):
        n = n1[:, None] + N1 * (128 * ch + m[None, :])
        pos[ch] = n if ch == 0 else (N - n)
    pos[1, 0, 0] = 0.0
    p = pos.reshape(-1)
    t = p / max(L - 1, 1)
    bands = np.linspace(1e-4, 15.0, 16)
    ang = (TWO_PI * p / L)[:, None] * bands[None, :]
    z = np.concatenate([t[:, None], np.cos(ang), -np.sin(ang)], axis=1)
    zT = np.ascontiguousarray(z.T, dtype=np.float32)
    min_decay = math.log(1e-2) / 1.5
    max_decay = math.log(1e-2) / 0.3
    deltas = np.abs(np.linspace(min_decay, max_decay, 1024))
    tneg = np.zeros((128, 2, N1), np.float32)
    for ch in range(2):
        tneg[:, ch, :] = -(pos[ch].T) / (L - 1.0)
    return zT, deltas.astype(np.float32)[None, :], tneg.reshape(128, 2 * N1)


def _rope_tables(positions):
    pos = positions.astype(np.float32)
    inv = (np.float32(10000.0) ** (-(np.arange(0, 64, 2, dtype=np.float32)) / np.float32(64))).astype(np.float32)
    ang = (pos[:, None] * inv[None, :]).astype(np.float32)
    c = np.cos(ang).T
    s = np.sin(ang).T
    return np.ascontiguousarray(np.concatenate([c, c, s, s], axis=0), dtype=np.float32)


class Builder:
    def __init__(self, groups=(0, 1), dbg=False):
        self.nc = bass.Bass("TRN2", target_bir_lowering=False)
        self.P = Prog()
        self.groups = groups
        self.dbg = dbg
        self.din = {}
        self.dout = {}
        self.uid = 0

    def inp(self, name, shape, dt=F32):
        self.din[name] = self.nc.dram_tensor(name, list(shape), dt, kind="ExternalInput").ap()
        return self.din[name]

    def outp(self, name, shape, dt=F32):
        self.dout[name] = self.nc.dram_tensor(name, list(shape), dt, kind="ExternalOutput").ap()
        return self.dout[name]

    def scratch(self, name, shape, dt):
        return self.nc.dram_tensor(name, list(shape), dt).ap()

    def sb(self, es, name, shape, dt):
        self.uid += 1
        return es.enter_context(self.nc.sbuf_tensor(f"{name}_{self.uid}", list(shape), dt))

    def psum(self, es, name, shape, dt):
        self.uid += 1
        return es.enter_context(self.nc.psum_tensor(f"{name}_{self.uid}", list(shape), dt))

    def mm(self, out, lhsT, rhs, start, stop, r, w):
        self.P.op("pe", lambda e: e.matmul(out, lhsT=lhsT, rhs=rhs, start=start, stop=stop), r=r, w=w)

    def dma(self, out, in_, r, w, slow=False):
        if slow:
            self.P.dma(lambda e: e.dma_start(out=out, in_=in_, allow_slow_non_contiguous=True), r=r, w=w)
        else:
            self.P.dma(lambda e: e.dma_start(out=out, in_=in_), r=r, w=w)

    def tt(self, eng, out, in0, in1, op, r, w):
        self.P.op(eng, lambda e: e.tensor_tensor(out=out, in0=in0, in1=in1, op=op), r=r, w=w)

    def ts(self, eng, out, in0, s1, s2, op0, op1, r, w):
        if op1 is None:
            self.P.op(eng, lambda e: e.tensor_scalar(out=out, in0=in0, scalar1=s1, scalar2=None, op0=op0), r=r, w=w)
        else:
            self.P.op(eng, lambda e: e.tensor_scalar(out=out, in0=in0, scalar1=s1, scalar2=s2, op0=op0, op1=op1), r=r, w=w)

    def stt(self, eng, out, in0, scalar, in1, op0, op1, r, w):
        self.P.op(eng, lambda e: e.scalar_tensor_tensor(out=out, in0=in0, scalar=scalar, in1=in1, op0=op0, op1=op1), r=r, w=w)

    def act(self, out, in_, func, r, w, bias=None, scale=None):
        kw = {}
        if bias is not None:
            kw["bias"] = bias
        if scale is not None:
            kw["scale"] = scale
        self.P.op("act", lambda e: e.activation(out=out, in_=in_, func=func, **kw), r=r, w=w)

    def copy(self, eng, out, in_, r, w):
        if eng == "act":
            self.P.op("act", lambda e: e.copy(out=out, in_=in_), r=r, w=w)
        else:
            self.P.op(eng, lambda e: e.tensor_copy(out=out, in_=in_), r=r, w=w)

    def memset(self, eng, ap, val, w):
        self.P.op(eng, lambda e: e.memset(ap, val), w=w)

    def load_w(self, dst, src, nk, ncols, key, off=0, eng="pool"):
        for kc in range(nk):
            c0 = 0
            while c0 < ncols:
                wdt = min(1024, ncols - c0)
                s = self.stg_i % 2
                self.stg_i += 1
                st = self.stg[s]
                self.dma(st[:, 0:wdt], src[kc * 128:(kc + 1) * 128, c0:c0 + wdt], r=[], w=[("stg", s)])
                self.copy(eng, dst[:, kc, off + c0:off + c0 + wdt], st[:, 0:wdt], r=[("stg", s)], w=[key])
                c0 += wdt

    def layer_norm(self, src, out, n, gbc, bbc, key_src, key_out, tmpk):
        st, mv = self.ln_st, self.ln_mv
        P = self.P
        for c in range(2):
            P.op("dve", lambda e, c=c: e.bn_stats(out=st[:n, c * 6:(c + 1) * 6], in_=src[:n, c * 512:(c + 1) * 512]), r=[key_src], w=["ln_st"])
        P.op("dve", lambda e: e.bn_aggr(out=mv[:n, 0:2], in_=st[:n, 0:12]), r=["ln_st"], w=["ln_mv"])
        self.act(mv[:n, 2:3], mv[:n, 1:2], AF.Sqrt, r=["ln_mv", "epsc"], w=["ln_mv2"], bias=self.epsc[:n, 0:1], scale=1.0)
        P.op("dve", lambda e: e.reciprocal(out=mv[:n, 3:4], in_=mv[:n, 2:3]), r=["ln_mv2"], w=["ln_mv3"])
        self.ts("dve", src[:n, :], src[:n, :], mv[:n, 0:1], mv[:n, 3:4], ALU.subtract, ALU.mult, r=[key_src, "ln_mv", "ln_mv3"], w=[key_src])
        self.tt("pool", src[:n, :], src[:n, :], gbc[:n, :], ALU.mult, r=[key_src, "lnp"], w=[key_src])
        self.tt("pool", out[:n, :], src[:n, :], bbc[:n, :], ALU.add, r=[key_src, "lnp"], w=[key_out])

    def build(self):
        nc, P = self.nc, self.P
        W = {}
        for name, shape in [
            ("w_in", (1, 1024, 5824)), ("short_w", (1, 3, 3072)), ("short_b", (1, 3072)), ("q_norm_g", (1, 384)),
            ("w_uq", (1, 384, 1536)), ("kv_norm_g", (1, 256)), ("w_ukv", (1, 256, 2048)), ("w_o_mla", (1, 1024, 1024)),
            ("filt_w1", (1, 33, 64)), ("filt_b1", (1, 64)), ("filt_freq", (1, 64)), ("filt_w2", (1, 64, 64)),
            ("filt_b2", (1, 64)), ("filt_w3", (1, 64, 4096)), ("hy_skip", (1, 2, 1024)), ("w_o_hy", (1, 1024, 1024)),
            ("w_out", (1, 1024, 1024)), ("ln1_g", (1, 1024)), ("ln1_b", (1, 1024)), ("w_ffn_up", (1, 1024, 5632)),
            ("dw_w", (1, 3, 2816)), ("dw_b", (1, 2816)), ("w_ffn_down", (1, 2816, 1024)), ("ln2_g", (1, 1024)), ("ln2_b", (1, 1024)),
        ]:
            W[name] = self.inp(name, shape)
        self.W = W
        C = {}
        C["ident"] = self.inp("c_ident", (128, 128))
        C["stack2"] = self.inp("c_stack2", (128, 128))
        C["delta"] = self.inp("c_delta", (1, 1024))
        GI = {}
        for G in GROUPS:
            g = G["g"]
            d = {}
            d["xtp"] = self.inp(f"xtp{g}", (1024, G["NB"] * 128))
            d["xqT"] = self.inp(f"xqT{g}", (1024, G["NQX"]))
            d["xq"] = self.inp(f"xq{g}", (G["NQX"], 1024))
            d["ropek"] = self.inp(f"ropek{g}", (128, G["L"]))
            d["ropeq"] = self.inp(f"ropeq{g}", (128, G["NQX"]))
            d["zemb"] = self.inp(f"zemb{g}", (33, 2 * G["NP"]))
            d["tneg"] = self.inp(f"tneg{g}", (128, 2 * G["N1"]))
            d["sel"] = self.inp(f"sel{g}", (128, G["NSEL"]))
            d["hm"] = self.inp(f"hm{g}", (128, 2))
            for k, n in [("f2", 512), ("twa", 512), ("twb", 512), ("f1", 512), ("r12", 768), ("ita", 512), ("itb", 512), ("ics", 384)]:
                d[k] = self.inp(f"{k}{g}", (128, n))
            d["y"] = self.outp(f"y{g}", (G["NQ"], 1024))
            d["xtb"] = self.scratch(f"s_xtb{g}", (1024, G["NB"] * 128), BF16)
            d["kf"] = self.scratch(f"s_kf{g}", (G["NSC"] * 2 * 128, 16 * 256), BF16)
            d["ot"] = self.scratch(f"s_ot{g}", (1024, G["NQX"]), BF16)
            d["kts"] = self.scratch(f"s_kts{g}", (1024, G["L"]), BF16)
            d["vts"] = self.scratch(f"s_vts{g}", (1024, G["L"]), BF16)
            d["zt"] = self.scratch(f"s_zt{g}", (1024, G["NQX"]), BF16)
            d["y1s"] = self.scratch(f"s_y1{g}", (G["NQX"], 1024), F32)
            d["y1t"] = self.scratch(f"s_y1t{g}", (1024, G["NQX"]), BF16)
            GI[g] = d
        self.GI = GI
        self.wupb = self.scratch("s_wup", (1024, 5632), BF16)
        self.wdnb = self.scratch("s_wdn", (2816, 1024), BF16)
        self.whyb = self.scratch("s_why", (1024, 3072), BF16)

        with ExitStack() as es:
            self.sems = {e: es.enter_context(nc.semaphore(f"s_{e}")) for e in ENGS}
            self.dsems = [es.enter_context(nc.semaphore(f"d_{i}")) for i in range(NDMA_SEM)]
            self.stg = [self.sb(es, f"stg{i}", (128, 1024), F32) for i in range(2)]
            self.stg_i = 0
            self.identb = self.sb(es, "identb", (128, 128), BF16)
            self.identf = self.sb(es, "identf", (128, 128), F32)
            self.stack2 = self.sb(es, "stack2", (128, 128), BF16)
            self.ones = self.sb(es, "ones", (128, 128), BF16)
            self.epsc = self.sb(es, "epsc", (128, 2), F32)
            self.ln_st = self.sb(es, "ln_st", (128, 12), F32)
            self.ln_mv = self.sb(es, "ln_mv", (128, 4), F32)
            self.dma(self.identf[:], C["ident"], r=[], w=["identf"])
            self.copy("pool", self.identb[:], self.identf[:], r=["identf"], w=["identb"])
            self.dma(self.stg[0][:, 0:128], C["stack2"], r=[], w=[("stg", 0)])
            self.copy("pool", self.stack2[:], self.stg[0][:, 0:128], r=[("stg", 0)], w=["stack2"])
            self.memset("pool", self.ones[:], 1.0, w=["ones"])
            self.memset("pool", self.epsc[:, 0:1], LN_EPS, w=["epsc"])
            self.memset("pool", self.epsc[:, 1:2], RMS_EPS, w=["epsc"])

            self.phase_w()
            for G in GROUPS:
                if G["g"] not in self.groups:
                    continue
                self.phase_filters(G)
                self.phase_attn(G)
                self.phase_hyena(G)
                self.phase_c1(G)
            block = es.enter_context(nc.Block())
            cnt = P.analyze()
            n = P.emit(self.sems, self.dsems, block)
            self.stats = (cnt, n)
        return nc

    def phase_w(self):
        P = self.P
        with ExitStack() as es:
            wb = [self.sb(es, f"wcast{i}", (128, 1024), BF16) for i in range(2)]
            i = 0
            for (src, dst, rows, cols) in [(self.W["w_in"][0][:, IN_HY0:IN_HY0 + 3072], self.whyb, 1024, 3072),
                                           (self.W["w_ffn_up"][0], self.wupb, 1024, 5632), (self.W["w_ffn_down"][0], self.wdnb, 2816, 1024)]:
                for r0 in range(0, rows, 128):
                    c0 = 0
                    while c0 < cols:
                        wdt = min(1024, cols - c0)
                        s = self.stg_i % 2
                        self.stg_i += 1
                        b = i % 2
                        i += 1
                        self.dma(self.stg[s][:, 0:wdt], src[r0:r0 + 128, c0:c0 + wdt], r=[], w=[("stg", s)])
                        self.copy("pool" if b else "act", wb[b][:, 0:wdt], self.stg[s][:, 0:wdt], r=[("stg", s)], w=[("wcast", b)])
                        self.dma(dst[r0:r0 + 128, c0:c0 + wdt], wb[b][:, 0:wdt], r=[("wcast", b)], w=["wscr"])
                        c0 += wdt
            P.barrier()

    def load_fft_tables(self, es, G):
        d = self.GI[G["g"]]
        T = {}
        T["f2"] = self.sb(es, "t_f2", (128, 2, 256), BF16)
        T["twa"] = self.sb(es, "t_twa", (128, 512), F32)
        T["twb"] = self.sb(es, "t_twb", (128, 512), F32)
        T["f1"] = self.sb(es, "t_f1", (128, 4, 128), BF16)
        T["r12"] = self.sb(es, "t_r12", (128, 3, 256), BF16)
        T["ita"] = self.sb(es, "t_ita", (128, 512), F32)
        T["itb"] = self.sb(es, "t_itb", (128, 512), F32)
        T["ics"] = self.sb(es, "t_ics", (128, 3, 128), BF16)
        for k in ("twa", "twb", "ita", "itb"):
            self.dma(T[k][:], d[k], r=[], w=["ffttab"])
        for k, n in (("f2", 512), ("f1", 512), ("r12", 768), ("ics", 384)):
            s = self.stg_i % 2
            self.stg_i += 1
            self.dma(self.stg[s][:, 0:n], d[k], r=[], w=[("stg", s)])
            self.copy("pool", T[k][:].rearrange("p a b -> p (a b)"), self.stg[s][:, 0:n], r=[("stg", s)], w=["ffttab"])
        return T

    def fft_bufs(self, es, inverse):
        Bf = {}
        Bf["ta"] = [self.sb(es, f"f_ta{i}", (128, 512), BF16) for i in range(2)]
        Bf["tb"] = [self.sb(es, f"f_tb{i}", (128, 512), BF16) for i in range(2)]
        Bf["araw"] = [self.sb(es, f"f_araw{i}", (128, 512), BF16) for i in range(2)]
        if inverse:
            Bf["tai"] = [self.sb(es, f"f_tai{i}", (128, 512), BF16) for i in range(2)]
            Bf["tbi"] = [self.sb(es, f"f_tbi{i}", (128, 512), BF16) for i in range(2)]
            Bf["e"] = [[self.sb(es, f"f_e{i}_{k}", (128, 2, 128), BF16) for k in range(4)] for i in range(2)]
            Bf["g"] = [[self.sb(es, f"f_g{i}_{k}", (128, 256), F32) for k in range(2)] for i in range(2)]
        return Bf

    @staticmethod
    def skew_emit(items):
        ns = max(len(st) for st in items) if items else 0
        n = len(items)
        for t in range(n + ns - 1):
            for sidx in range(ns - 1, -1, -1):
                i = t - sidx
                if 0 <= i < n and sidx < len(items[i]) and items[i][sidx] is not None:
                    items[i][sidx]()

    def fwd_stages(self, T, Bf, bank, ii, x_of, nch2, xkeys):
        s2 = ii % 2
        b1, b3 = bank(0), bank(2 + ii % 2)
        k1, k3 = ("pb", 0), ("pb", 2 + ii % 2)
        ta, tb = Bf["ta"][s2], Bf["tb"][s2]
        araw = Bf["araw"][s2]

        def F0():
            for u in range(2):
                for ch2 in range(nch2):
                    self.mm(b1[:, u * 256:(u + 1) * 256], x_of(u, ch2), T["f2"][:, ch2, :], ch2 == 0, ch2 == nch2 - 1,
                            r=list(xkeys) + ["ffttab"], w=[k1])

        def F0c():
            self.copy("act", araw[:], b1, r=[k1], w=[("araw", s2)])

        def F1():
            self.tt("dve", ta[:], araw[:], T["twa"][:], ALU.mult, r=[("araw", s2), "ffttab"], w=[("ta", s2)])
            self.tt("pool", tb[:], araw[:], T["twb"][:], ALU.mult, r=[("araw", s2), "ffttab"], w=[("tb", s2)])

        def F3():
            ta4 = ta[:].rearrange("p (u r k) -> p u r k", u=2, r=2)
            tb4 = tb[:].rearrange("p (u r k) -> p u r k", u=2, r=2)
            Cm, Sm, nSm, nCm = (T["f1"][:, i, :] for i in range(4))
            rr = [("ta", s2), ("tb", s2), "ffttab"]
            o = b3[:, 0:256]
            self.mm(o, Cm, ta4[:, :, 0, :], True, False, r=rr, w=[k3])
            self.mm(o, nCm, tb4[:, :, 1, :], False, False, r=rr, w=[k3])
            self.mm(o, Sm, tb4[:, :, 0, :], False, False, r=rr, w=[k3])
            self.mm(o, Sm, ta4[:, :, 1, :], False, True, r=rr, w=[k3])
            o = b3[:, 256:512]
            self.mm(o, Cm, tb4[:, :, 0, :], True, False, r=rr, w=[k3])
            self.mm(o, Cm, ta4[:, :, 1, :], False, False, r=rr, w=[k3])
            self.mm(o, nSm, ta4[:, :, 0, :], False, False, r=rr, w=[k3])
            self.mm(o, Sm, tb4[:, :, 1, :], False, True, r=rr, w=[k3])

        return [F0, F0c, F1, F3], b3, k3

    def inv_stages(self, T, Bf, bank, ii, b3, k3, Kp, kkey, cur_pair, cur_key, gate_pair, gate_key, skip_bc, dst_pair, dst_key, N1, skey):
        s2 = ii % 2
        b5, b7 = bank(4 + ii % 2), bank(6)
        k5, k7 = ("pb", 4 + ii % 2), ("pb", 6)
        e = Bf["e"][s2]
        g = Bf["g"][s2]
        tai, tbi = Bf["tai"][s2], Bf["tbi"][s2]
        Xr = b3[:, 0:256].rearrange("p (u k) -> p u k", u=2)
        Xi = b3[:, 256:512].rearrange("p (u k) -> p u k", u=2)
        Kr, Ki = Kp[:, :, 0, :], Kp[:, :, 1, :]
        ek = [("e", s2, k) for k in range(4)]

        def F4():
            self.tt("dve", e[0][:], Xr, Kr, ALU.mult, r=[k3, kkey], w=[ek[0]])
            self.tt("dve", e[1][:], Xi, Ki, ALU.mult, r=[k3, kkey], w=[ek[1]])
            self.tt("dve", e[2][:], Xr, Ki, ALU.mult, r=[k3, kkey], w=[ek[2]])
            self.tt("dve", e[3][:], Xi, Kr, ALU.mult, r=[k3, kkey], w=[ek[3]])

        def I0():
            R1, R2, R3 = (T["r12"][:, i, :] for i in range(3))
            for u in range(2):
                o = b5[:, u * 256:(u + 1) * 256]
                self.mm(o, e[0][:, u, :], R1, True, False, r=ek + ["ffttab"], w=[k5])
                self.mm(o, e[1][:, u, :], R3, False, False, r=ek + ["ffttab"], w=[k5])
                self.mm(o, e[2][:, u, :], R2, False, False, r=ek + ["ffttab"], w=[k5])
                self.mm(o, e[3][:, u, :], R2, False, True, r=ek + ["ffttab"], w=[k5])

        def I1():
            self.tt("dve", tai[:], b5, T["ita"][:], ALU.mult, r=[k5, "ffttab"], w=[("tai", s2)])
            self.tt("dve", tbi[:], b5, T["itb"][:], ALU.mult, r=[k5, "ffttab"], w=[("tbi", s2)])

        def I3():
            ta4 = tai[:].rearrange("p (u r k) -> p u r k", u=2, r=2)
            tb4 = tbi[:].rearrange("p (u r k) -> p u r k", u=2, r=2)
            IC, ISn, nIC = (T["ics"][:, i, :] for i in range(3))
            rr = [("tai", s2), ("tbi", s2), "ffttab"]
            o = b7[:, 0:256]
            self.mm(o, IC, ta4[:, :, 0, :], True, False, r=rr, w=[k7])
            self.mm(o, nIC, tb4[:, :, 1, :], False, False, r=rr, w=[k7])
            self.mm(o, ISn, tb4[:, :, 0, :], False, False, r=rr, w=[k7])
            self.mm(o, ISn, ta4[:, :, 1, :], False, True, r=rr, w=[k7])

        def I4():
            self.tt("pool", g[0][:].rearrange("p (c n) -> p c n", n=N1), cur_pair.rearrange("p (c n) -> p c n", n=N1), skip_bc, ALU.mult,
                    r=[cur_key, skey], w=[("g", s2, 0)])
            self.tt("dve", g[1][:], b7[:, 0:256], g[0][:], ALU.add, r=[k7, ("g", s2, 0)], w=[("g", s2, 1)])
            self.tt("pool", dst_pair, g[1][:], gate_pair, ALU.mult, r=[("g", s2, 1), gate_key], w=[dst_key])

        return [F4, I0, I1, I3, I4]

    def phase_filters(self, G):
        P = self.P
        g, N1, L, NP, Jc, NSC = G["g"], G["N1"], G["L"], G["NP"], G["Jc"], G["NSC"]
        d = self.GI[g]
        W = self.W
        with ExitStack() as es:
            T = self.load_fft_tables(es, G)
            ps = self.psum(es, "ps0", (128, 4096), F32)
            bank = lambda i: ps[:, i * 512:(i + 1) * 512]
            Bf = self.fft_bufs(es, inverse=False)
            h2T = self.sb(es, "h2T", (128, NP), BF16)
            w1 = self.sb(es, "fw1", (33, 128), F32)
            w2 = self.sb(es, "fw2", (128, 128), F32)
            fp = self.sb(es, "fpar", (128, 8), F32)
            w3r = self.sb(es, "w3r", (128, 4096), BF16)
            w3s = self.sb(es, "w3s", (128, 2, NSC, 2, Jc), BF16)
            delta = self.sb(es, "delta", (128, 1024), F32)
            tneg = self.sb(es, "tneg", (128, 2, N1), F32)
            zst = [self.sb(es, f"zst{i}", (33, 512), F32) for i in range(2)]
            hA = self.sb(es, "hA", (128, 512), F32)
            hB = self.sb(es, "hB", (128, 512), F32)
            h1 = self.sb(es, "h1", (128, 512), F32)
            nb1 = 512 // (2 * Jc)
            argt = [self.sb(es, f"argt{i}", (128, nb1, Jc), F32) for i in range(2)]
            e2t = [self.sb(es, f"e2t{i}", (128, nb1, Jc), F32) for i in range(2)]
            kt = [self.sb(es, f"kt{i}", (128, 2, 2, Jc * N1), BF16) for i in range(2)]
            kfo = [self.sb(es, f"kfo{i}", (128, 16, 2, 128), BF16) for i in range(2)]
            psF = [bank(6), bank(7)]

            for h in range(2):
                self.dma(w1[:, h * 64:(h + 1) * 64], W["filt_w1"][0], r=[], w=["fw1"])
                self.dma(w2[0:64, h * 64:(h + 1) * 64], W["filt_w2"][0], r=[], w=["fw2"])
                for i, nm in enumerate(("filt_freq", "filt_b1", "filt_b2")):
                    self.dma(fp[h * 64:(h + 1) * 64, i:i + 1], W[nm][0].rearrange("(p o) -> p o", o=1), r=[], w=["fpar"], slow=True)
                for c0 in (0, 1024, 2048, 3072):
                    s = self.stg_i % 2
                    self.stg_i += 1
                    self.dma(self.stg[s][h * 64:(h + 1) * 64, :], W["filt_w3"][0][:, c0:c0 + 1024], r=[], w=[("stg", s)])
                    self.copy("pool", w3r[h * 64:(h + 1) * 64, c0:c0 + 1024], self.stg[s][h * 64:(h + 1) * 64, :], r=[("stg", s)], w=["w3r"])
            w3v = w3r[:].rearrange("p (d o s c) -> p d o s c", d=2, o=2, s=NSC)
            for dd in range(2):
                for o in range(2):
                    self.copy("pool", w3s[:, dd, :, o, :], w3v[:, dd, o, :, :], r=["w3r"], w=["w3s"])
            self.tt("dve", fp[:, 3:4], fp[:, 0:1], fp[:, 1:2], ALU.mult, r=["fpar"], w=["fpar2"])
            self.tt("dve", fp[:, 4:5], fp[:, 0:1], fp[:, 2:3], ALU.mult, r=["fpar"], w=["fpar2"])
            self.dma(delta[:], self.din["c_delta"].partition_broadcast(128), r=[], w=["delta"])
            self.dma(tneg[:].rearrange("p a b -> p (a b)"), d["tneg"], r=[], w=["tneg"])
            self.ts("pool", w3s[:, 1, :, :, :], w3s[:, 1, :, :, :], -1.0, None, ALU.mult, None, r=["w3s"], w=["w3s"])

            def sin_layer(psrc, fbcol, dst, lo, rk, wk):
                self.ts("dve", hA[:], psrc, fp[:, 0:1], fp[:, fbcol:fbcol + 1], ALU.mult, ALU.add, r=[rk, "fpar", "fpar2"], w=["hA"])
                self.ts("dve", hB[:], hA[:], 1.0 / TWO_PI, MAGIC, ALU.mult, ALU.add, r=["hA"], w=["hB"])
                self.ts("dve", hB[:], hB[:], MAGIC, -TWO_PI, ALU.subtract, ALU.mult, r=["hB"], w=["hB"])
                self.tt("dve", hA[:], hA[:], hB[:], ALU.add, r=["hA", "hB"], w=["hA"])
                n_ = dst.shape[0]
                self.act(dst, hA[lo:lo + n_, :], AF.Sin, r=["hA"], w=[wk], scale=0.999999)

            nblk = 2 * NP // 512
            for b in range(nblk):
                zb = b % 2
                half = (b * 512) // NP
                c0 = b * 512 - half * NP
                self.dma(zst[zb][:], d["zemb"][:, b * 512:(b + 1) * 512], r=[], w=[("zst", zb)])
                self.mm(psF[0], w1[:, :], zst[zb][:], True, True, r=["fw1", ("zst", zb)], w=[("pb", 6)])
                sin_layer(psF[0], 3, h1[:], 0, ("pb", 6), "h1")
                self.mm(psF[1], w2[0:64, :], h1[0:64, :], True, True, r=["fw2", "h1"], w=[("pb", 7)])
                lo = half * 64
                sin_layer(psF[1], 4, h2T[lo:lo + 64, c0:c0 + 512], lo, ("pb", 7), "h2T")
            self.memset("pool", h2T[64:128, 0:1], 0.0, w=["h2T"])

            step_ctr = [0]

            def ktgen_steps(sc):
                kb = sc % 2
                ch0 = sc * Jc
                steps = []
                for half in range(2):
                    hp = slice(64 * half, 64 * half + 64)
                    for nb in range(N1 // nb1):
                        c = step_ctr[0]
                        step_ctr[0] += 1
                        pf = 4 + c % 4
                        t2 = c % 2

                        def pre(half=half, hp=hp, nb=nb, pf=pf, t2=t2):
                            psv = bank(pf).rearrange("p (n o c) -> p n o c", n=nb1, o=2)
                            for i in range(nb1):
                                n1 = nb * nb1 + i
                                self.mm(psv[:, i, :, :], h2T[hp, n1 * 128:(n1 + 1) * 128], w3s[hp, half, sc, :, :], True, True,
                                        r=["h2T", "w3s"], w=[("pb", pf)])
                            self.tt("pool", argt[t2][:], tneg[:, half, nb * nb1:(nb + 1) * nb1].unsqueeze(2).to_broadcast([128, nb1, Jc]),
                                    delta[:, ch0:ch0 + Jc].unsqueeze(1).to_broadcast([128, nb1, Jc]), ALU.mult, r=["tneg", "delta"], w=[("argt", t2)])
                            self.act(e2t[t2][:], argt[t2][:], AF.Exp, r=[("argt", t2)], w=[("e2t", t2)])

                        def fin(half=half, nb=nb, pf=pf, t2=t2):
                            psv = bank(pf).rearrange("p (n o c) -> p n o c", n=nb1, o=2)
                            for o in range(2):
                                ktv = kt[kb][:, o, half, :].rearrange("p (c n) -> p c n", n=N1)[:, :, nb * nb1:(nb + 1) * nb1]
                                self.stt("dve", ktv, e2t[t2][:].rearrange("p n c -> p c n"), 0.05,
                                         psv[:, :, o, :].rearrange("p n c -> p c n"), ALU.add, ALU.mult,
                                         r=[("e2t", t2), ("pb", pf)], w=[("kt", kb)])
                        steps.append((pre, fin))
                return steps

            for (pre_, fin_) in ktgen_steps(0):
                pre_()
                fin_()
            ii = 0
            uc = 0
            for sc in range(NSC):
                kb = sc % 2
                nxt = ktgen_steps(sc + 1) if sc + 1 < NSC else []
                assert len(nxt) in (0, 16)
                items = []
                for o in range(2):
                    kr = uc % 2
                    uc += 1
                    for jp in range(8):
                        def x_of(u, ch2, kb=kb, o=o, jp=jp):
                            return kt[kb][:, o, ch2, (jp * 2 + u) * 128:(jp * 2 + u + 1) * 128]

                        st, b3, k3 = self.fwd_stages(T, Bf, bank, ii, x_of, 2, [("kt", kb)])
                        ii += 1
                        idx = o * 8 + jp
                        extra = []
                        if nxt:
                            if idx > 0:
                                extra.append(nxt[idx - 1][1])
                            extra.append(nxt[idx][0])
                        if extra:
                            f0 = st[0]

                            def F0x(f0=f0, extra=extra):
                                for ex in extra:
                                    ex()
                                f0()
                            st[0] = F0x

                        def F4c(b3=b3, k3=k3, kr=kr, jp=jp, sc=sc, o=o):
                            self.copy("act", kfo[kr][:, jp * 2:(jp + 1) * 2, :, :].rearrange("p u r k -> p r u k"),
                                      b3.rearrange("p (r u k) -> p r u k", r=2, u=2), r=[k3], w=[("kfo", kr)])
                            if jp == 7:
                                row = (sc * 2 + o) * 128
                                self.dma(d["kf"][row:row + 128, :], kfo[kr][:].rearrange("p a r k -> p (a r k)"), r=[("kfo", kr)], w=["kfscr"])
                        items.append(st + [F4c])
                self.skew_emit(items)
                if nxt:
                    nxt[15][1]()
            P.barrier()

    def phase_attn(self, G):
        P = self.P
        g, N1, L, NQX = G["g"], G["N1"], G["L"], G["NQX"]
        d = self.GI[g]
        W = self.W
        with ExitStack() as es:
            esp = ExitStack()
            ps = self.psum(es, "psA", (128, 4096), F32)
            bank = lambda i: ps[:, i * 512:(i + 1) * 512]
            wuq = self.sb(es, "wuq", (128, 3, 8, 256), BF16)
            TK = self.sb(es, "TK", (128, L), BF16)
            cqn = self.sb(es, "cqn", (128, 3, NQX), BF16)
            ropeq = self.sb(es, "ropeq", (128, NQX), F32)
            wkv = self.sb(esp, "wkv", (128, 8, 384), BF16)
            wqi = self.sb(esp, "wqi", (128, 8, 384), BF16)
            wukv = self.sb(esp, "wukv", (128, 2, 2048), BF16)
            gkv = self.sb(esp, "gkv", (128, 2), F32)
            gq = self.sb(esp, "gq", (128, 3), F32)
            ckt = [self.sb(esp, f"ckt{i}", (128, 2, 512), BF16) for i in range(2)]
            kst = [self.sb(esp, f"kst{i}", (128, 512), BF16) for i in range(2)]
            tkt = [self.sb(esp, f"tkt{i}", (128, 512), BF16) for i in range(2)]
            vst = [self.sb(esp, f"vst{i}", (128, 512), BF16) for i in range(2)]
            xs32 = [self.sb(esp, f"xs32_{i}", (128, 8, 512), F32) for i in range(2)]
            xsb = [self.sb(esp, f"xsb_{i}", (128, 8, 512), BF16) for i in range(2)]
            rk = [self.sb(esp, f"rk{i}", (128, 512), F32) for i in range(2)]
            sq = self.sb(esp, "sq", (128, 3, 512), BF16)
            rstd = self.sb(esp, "rstd", (128, 512), F32)

            w_in = W["w_in"][0]
            self.load_w(wkv, w_in[:, IN_KV0:IN_KV0 + 320], 8, 320, "wkv")
            self.load_w(wqi, w_in[:, 0:384], 8, 384, "wqi")
            self.ts("pool", wkv[:, :, 320:352], wkv[:, :, 288:320], -1.0, None, ALU.mult, None, r=["wkv"], w=["wkv"])
            self.copy("pool", wkv[:, :, 352:384], wkv[:, :, 256:288], r=["wkv"], w=["wkv"])
            for m in range(3):
                for h in range(NH):
                    s = self.stg_i % 2
                    self.stg_i += 1
                    self.dma(self.stg[s][:, 0:192], W["w_uq"][0][m * 128:(m + 1) * 128, h * 192:(h + 1) * 192], r=[], w=[("stg", s)])
                    self.copy("pool", wuq[:, m, h, 0:192], self.stg[s][:, 0:192], r=[("stg", s)], w=["wuq"])
            self.ts("pool", wuq[:, :, :, 192:224], wuq[:, :, :, 160:192], -1.0, None, ALU.mult, None, r=["wuq"], w=["wuq"])
            self.copy("pool", wuq[:, :, :, 224:256], wuq[:, :, :, 128:160], r=["wuq"], w=["wuq"])
            self.load_w(wukv, W["w_ukv"][0], 2, 2048, "wukv")
            self.dma(gkv[:], W["kv_norm_g"][0].rearrange("(a p) -> p a", p=128), r=[], w=["gkv"], slow=True)
            self.dma(gq[:], W["q_norm_g"][0].rearrange("(a p) -> p a", p=128), r=[], w=["gq"], slow=True)
            self.dma(ropeq[:], d["ropeq"], r=[], w=["ropeq"])

            xv = d["xtp"].rearrange("(kc kp) c -> kp kc c", kp=128)
            xbv = d["xtb"].rearrange("(kc kp) c -> kp kc c", kp=128)
            chunks = [(0, 128)] + [(128 + i * 512, 128 + (i + 1) * 512) for i in range(L // 512)] + [(128 + L, 256 + L)]

            def rms_to(psums, nchunk, n, gcol, dst_of, inv_dim, pss, keyset, dkey):
                for m in range(nchunk):
                    self.act(sq[:, m, 0:n], psums[m], AF.Square, r=[keyset[m]], w=["sq"])
                for m in range(nchunk):
                    self.mm(pss[:, 0:n], self.ones[:], sq[:, m, 0:n], m == 0, m == nchunk - 1, r=["sq", "ones"], w=[keyset[-1]])
                self.act(rstd[:, 0:n], pss[:, 0:n], AF.Sqrt, r=[keyset[-1], "epsc"], w=["rstd"], bias=self.epsc[:, 1:2], scale=inv_dim)
                P.op("dve", lambda e: e.reciprocal(out=rstd[:, 0:n], in_=rstd[:, 0:n]), r=["rstd"], w=["rstd"])
                for m in range(nchunk):
                    self.stt("dve", dst_of(m), psums[m], gcol[:, m:m + 1], rstd[:, 0:n], ALU.mult, ALU.mult,
                             r=[keyset[m], "rstd", "gkv", "gq"], w=[dkey])

            for ci, (c0, c1) in enumerate(chunks):
                n = c1 - c0
                xb = ci % 2
                self.dma(xs32[xb][:, :, 0:n], xv[:, :, c0:c1], r=[], w=[("xs32", xb)])
                self.copy("pool" if ci % 2 else "act", xsb[xb][:, :, 0:n], xs32[xb][:, :, 0:n], r=[("xs32", xb)], w=[("xsb", xb)])
                self.dma(xbv[:, :, c0:c1], xsb[xb][:, :, 0:n], r=[("xsb", xb)], w=["xtbscr"])
                if 128 <= c0 < 128 + L:
                    k0 = c0 - 128
                    cb = ci % 2
                    pk = [("psk", i) for i in range(4)]
                    for mi, (cs, ce) in enumerate([(0, 128), (128, 256), (256, 384)]):
                        for kc in range(8):
                            self.mm(bank(mi), wkv[:, kc, cs:ce], xsb[xb][:, kc, 0:512], kc == 0, kc == 7, r=["wkv", ("xsb", xb)], w=[pk[mi]])
                    self.dma(rk[xb][:], d["ropek"][:, k0:k0 + 512], r=[], w=[("rk", xb)])
                    rms_to([bank(0), bank(1)], 2, 512, gkv, lambda m: ckt[cb][:, m, :], 1.0 / KVL, bank(3), [pk[0], pk[1], pk[3]], ("ckt", cb))
                    self.tt("dve", tkt[cb][:], bank(2), rk[xb][:], ALU.mult, r=[pk[2], ("rk", xb)], w=[("tkt", cb)])
                    self.mm(bank(2), self.stack2[:], tkt[cb][:], True, True, r=["stack2", ("tkt", cb)], w=[pk[2]])
                    self.copy("act", TK[:, k0:k0 + 512], bank(2), r=[pk[2]], w=["TK"])
                    for h in range(NH):
                        pbk = 4 + 2 * (h % 2)
                        sk = (ci * NH + h) % 2
                        for a in range(2):
                            self.mm(bank(pbk), wukv[:, a, h * 256:h * 256 + 128], ckt[cb][:, a, :], a == 0, a == 1,
                                    r=["wukv", ("ckt", cb)], w=[("psk", pbk)])
                        self.copy("act", kst[sk][:], bank(pbk), r=[("psk", pbk)], w=[("kst", sk)])
                        self.dma(d["kts"][h * 128:(h + 1) * 128, k0:k0 + 512], kst[sk][:], r=[("kst", sk)], w=["ktscr"])
                        for u in range(4):
                            for a in range(2):
                                self.mm(bank(pbk + 1)[:, u * 128:(u + 1) * 128], ckt[cb][:, a, u * 128:(u + 1) * 128],
                                        wukv[:, a, h * 256 + 128:h * 256 + 256], a == 0, a == 1, r=["wukv", ("ckt", cb)], w=[("psk", pbk + 1)])
                        self.copy("dve" if h % 2 else "act", vst[sk][:], bank(pbk + 1), r=[("psk", pbk + 1)], w=[("vst", sk)])
                        self.dma(d["vts"][h * 128:(h + 1) * 128, k0:k0 + 512], vst[sk][:], r=[("vst", sk)], w=["vtscr"])

            xqv = d["xqT"].rearrange("(kc kp) c -> kp kc c", kp=128)
            for qi, q0 in enumerate(range(0, NQX, 512)):
                n = min(512, NQX - q0)
                xb = qi % 2
                self.dma(xs32[xb][:, :, 0:n], xqv[:, :, q0:q0 + n], r=[], w=[("xs32", xb)])
                self.copy("pool", xsb[xb][:, :, 0:n], xs32[xb][:, :, 0:n], r=[("xs32", xb)], w=[("xsb", xb)])
                bs = 4 * (qi % 2)
                pk = [("psk", bs + i) for i in range(4)]
                for mi in range(3):
                    for kc in range(8):
                        self.mm(bank(bs + mi)[:, 0:n], wqi[:, kc, mi * 128:(mi + 1) * 128], xsb[xb][:, kc, 0:n], kc == 0, kc == 7,
                                r=["wqi", ("xsb", xb)], w=[pk[mi]])
                rms_to([bank(bs + i)[:, 0:n] for i in range(3)], 3, n, gq, lambda m: cqn[:, m, q0:q0 + n], 1.0 / QL, bank(bs + 3), pk, "cqn")
            P.barrier()
            esp.close()

            with ExitStack() as es2:
                KT = self.sb(es2, "KT", (128, L), BF16)
                V = self.sb(es2, "V", (128, L // 128, 128), BF16)
                QN = self.sb(es2, "QN", (128, NQX), BF16)
                QR = self.sb(es2, "QR", (128, NQX), BF16)
                tq = self.sb(es2, "tq", (128, 512), BF16)
                PT = [self.sb(es2, f"PT{i}", (128, 1024), BF16) for i in range(2)]
                acc = self.sb(es2, "acc", (128, 1024), F32)
                accb = self.sb(es2, "accb", (128, 1024), BF16)
                rs = self.sb(es2, "rs", (128, 1024), F32)
                osb = [self.sb(es2, f"osb{i}", (128, 1024), BF16) for i in range(2)]
                Sb = [ps[:, 0:1024], ps[:, 1024:2048]]
                Ob = ps[:, 2048:3072]
                m6, m7 = bank(6), bank(7)
                nkt = L // 128
                oi = 0
                for h in range(NH):
                    self.dma(KT[:], d["kts"][h * 128:(h + 1) * 128, :], r=["ktscr"], w=["KT"])
                    self.dma(V[:].rearrange("p t d -> p (t d)"), d["vts"][h * 128:(h + 1) * 128, :], r=["vtscr"], w=["V"])
                    for qi, q0 in enumerate(range(0, NQX, 512)):
                        n = min(512, NQX - q0)
                        for m in range(3):
                            self.mm(m6[:, 0:n], wuq[:, m, h, 0:128], cqn[:, m, q0:q0 + n], m == 0, m == 2, r=["wuq", "cqn"], w=[("pm", 6)])
                        self.copy("act", QN[:, q0:q0 + n], m6[:, 0:n], r=[("pm", 6)], w=["QN"])
                        for m in range(3):
                            self.mm(m7[:, 0:n], wuq[:, m, h, 128:256], cqn[:, m, q0:q0 + n], m == 0, m == 2, r=["wuq", "cqn"], w=[("pm", 7)])
                        self.tt("dve", tq[:, 0:n], m7[:, 0:n], ropeq[:, q0:q0 + n], ALU.mult, r=[("pm", 7), "ropeq"], w=["tq"])
                        self.mm(m6[:, 0:n], self.stack2[:], tq[:, 0:n], True, True, r=["stack2", "tq"], w=[("pm", 6)])
                        self.copy("act", QR[:, q0:q0 + n], m6[:, 0:n], r=[("pm", 6)], w=["QR"])
                    for q0 in range(0, NQX, 1024):
                        nq = min(1024, NQX - q0)
                        subs = [(s0, min(512, nq - s0)) for s0 in range(0, nq, 512)]
                        self.memset("pool", acc[:, 0:nq], 0.0, w=["acc"])

                        def qk(kt_):
                            sbk = kt_ % 2
                            for (s0, sn) in subs:
                                o = Sb[sbk][:, s0:s0 + sn]
                                self.mm(o, KT[:, kt_ * 128:(kt_ + 1) * 128], QN[:, q0 + s0:q0 + s0 + sn], True, False, r=["KT", "QN"], w=[("S", sbk)])
                            for si, (s0, sn) in enumerate(subs):
                                o = Sb[sbk][:, s0:s0 + sn]
                                hp_ = slice(64 * (si % 2), 64 * (si % 2) + 64)
                                self.mm(o, TK[hp_, kt_ * 128:(kt_ + 1) * 128], QR[hp_, q0 + s0:q0 + s0 + sn], False, True, r=["TK", "QR"], w=[("S", sbk)])

                        qk(0)
                        for kt_ in range(nkt):
                            sbk = kt_ % 2
                            if kt_ + 1 < nkt:
                                qk(kt_ + 1)
                            self.act(PT[sbk][:, 0:nq], Sb[sbk][:, 0:nq], AF.Exp, r=[("S", sbk)], w=[("PT", sbk)], scale=ATT_SCALE)
                            self.tt("dve", acc[:, 0:nq], acc[:, 0:nq], PT[sbk][:, 0:nq], ALU.add, r=["acc", ("PT", sbk)], w=["acc"])
                            for (s0, sn) in subs:
                                self.mm(Ob[:, s0:s0 + sn], V[:, kt_, :], PT[sbk][:, s0:s0 + sn], kt_ == 0, kt_ == nkt - 1, r=["V", ("PT", sbk)], w=["O"])
                        self.copy("pool", accb[:, 0:nq], acc[:, 0:nq], r=["acc"], w=["accb"])
                        for (s0, sn) in subs:
                            self.mm(Sb[0][:, s0:s0 + sn], self.ones[:], accb[:, s0:s0 + sn], True, True, r=["ones", "accb"], w=[("S", 0)])
                        P.op("dve", lambda e, nq=nq: e.reciprocal(out=rs[:, 0:nq], in_=Sb[0][:, 0:nq]), r=[("S", 0)], w=["rs"])
                        ob = oi % 2
                        oi += 1
                        self.tt("dve", osb[ob][:, 0:nq], Ob[:, 0:nq], rs[:, 0:nq], ALU.mult, r=["O", "rs"], w=[("osb", ob)])
                        self.dma(d["ot"][h * 128:(h + 1) * 128, q0:q0 + nq], osb[ob][:, 0:nq], r=[("osb", ob)], w=["otscr"])
            P.barrier()

    def phase_hyena(self, G):
        P = self.P
        g, N1, L, NQX, JI, Jc, NB, NSEL = G["g"], G["N1"], G["L"], G["NQX"], G["JI"], G["Jc"], G["NB"], G["NSEL"]
        d = self.GI[g]
        W = self.W
        w_in = W["w_in"][0]
        nsub = JI // Jc
        with ExitStack() as es:
            T = self.load_fft_tables(es, G)
            wstat = (3 * JI <= 128)
            if wstat:
                ps = self.psum(es, "psH", (128, 3584), F32)
                pst = self.psum(es, "psHT", (128, 1024), BF16)
            else:
                ps = self.psum(es, "psH", (128, 4096), F32)
            bank = lambda i: ps[:, i * 512:(i + 1) * 512]
            Bf = self.fft_bufs(es, inverse=True)
            kfb = [self.sb(es, f"kfb{i}", (128, 16, 2, 128), BF16) for i in range(2)]
            whc = [self.sb(es, f"whc{i}", (128, 8, 3 * JI), BF16) for i in range(2)]
            taps = [self.sb(es, f"taps{i}", (128, 3, 3, JI), F32) for i in range(2)]
            tbias = [self.sb(es, f"tbias{i}", (128, 3, JI), F32) for i in range(2)]
            skp = [self.sb(es, f"skp{i}", (128, 2, JI), F32) for i in range(2)]
            uraw = [self.sb(es, f"uraw{i}", (128, 3 * JI, NB), BF16) for i in range(2)]
            xc = [[self.sb(es, f"xc{r_}_{i}", (128, Jc * N1), BF16) for i in range(3)] for r_ in range(2)]
            z1 = [self.sb(es, f"z1_{r_}", (128, Jc * N1), BF16) for r_ in range(2)]
            z2 = self.sb(es, "z2", (128, JI * N1), BF16)
            nhh = 2 if Jc <= 16 else 4
            Jh = Jc // nhh
            cacc = self.sb(es, "cacc", (128, Jh, N1), F32)
            ctmp = self.sb(es, "ctmp", (128, Jh, N1), F32)
            XW = 512 if N1 == 128 else 128
            xbl = [self.sb(es, f"xbl{i}", (128, 8, XW), BF16) for i in range(2)]
            uT = [self.sb(es, f"uT{i}", (128, 512), BF16) for i in range(2)] if 3 * JI <= 128 else None
            whv = self.whyb.rearrange("(kc kp) c -> kp kc c", kp=128)
            sel = self.sb(es, "sel", (128, NSEL), BF16)
            MZ = min(128, JI)
            ztc = self.sb(es, "ztc", (MZ, NSEL, N1), BF16)
            self.dma(self.stg[0][:, 0:NSEL], d["sel"], r=[], w=[("stg", 0)])
            self.copy("pool", sel[:], self.stg[0][:, 0:NSEL], r=[("stg", 0)], w=["sel"])
            xbv = d["xtb"].rearrange("(kc kp) c -> kp kc c", kp=128)
            ncolb = NB * 128
            ii = 0
            nchan = 256 // N1
            wstat_dummy = False
            assert nsub == 2 and 3 * JI <= 512
            ld_ctr = [0]

            def inproj_steps(cj):
                rb = cj % 2
                cc0 = cj * JI
                steps = []

                def params():
                    for t in range(3):
                        self.dma(whc[rb][:, :, t * JI:(t + 1) * JI], whv[:, :, t * 1024 + cc0:t * 1024 + cc0 + JI], r=["wscr"], w=[("whc", rb)])
                        for tap in range(3):
                            self.dma(taps[rb][:, tap, t, :], W["short_w"][0][tap:tap + 1, t * 1024 + cc0:t * 1024 + cc0 + JI].partition_broadcast(128),
                                     r=[], w=[("taps", rb)])
                        self.dma(tbias[rb][:, t, :], W["short_b"][:, t * 1024 + cc0:t * 1024 + cc0 + JI].partition_broadcast(128), r=[], w=[("taps", rb)])
                    for o in range(2):
                        self.dma(skp[rb][:, o, :], W["hy_skip"][0][o:o + 1, cc0:cc0 + JI].partition_broadcast(128), r=[], w=[("skp", rb)])
                steps.append(params)
                for b0 in range(0, ncolb, XW):
                    def step(b0=b0):
                        nn = min(XW, ncolb - b0)
                        xb = ld_ctr[0] % 2
                        ld_ctr[0] += 1
                        self.dma(xbl[xb][:, :, 0:nn], xbv[:, :, b0:b0 + nn], r=["xtbscr"], w=[("xbl", xb)])
                        if wstat:
                            M3 = 3 * JI
                            nbk = nn // 128
                            blk0 = b0 // 128
                            for kc in range(8):
                                self.mm(bank(1)[0:M3, 0:nn], whc[rb][:, kc, :], xbl[xb][:, kc, 0:nn], kc == 0, kc == 7,
                                        r=[("xbl", xb), ("whc", rb)], w=[("pb", 1)])
                            self.copy("act", uT[xb][0:M3, 0:nn], bank(1)[0:M3, 0:nn], r=[("pb", 1)], w=[("uT", xb)])
                            for u in range(nbk):
                                P.op("pe", lambda e, u=u, xb=xb: e.transpose(pst[:, u * M3:(u + 1) * M3], uT[xb][0:M3, u * 128:(u + 1) * 128],
                                                                              self.identb[0:M3, 0:M3]),
                                     r=[("uT", xb), "identb"], w=[("pb", 7)])
                            self.copy("act", uraw[rb][:, :, blk0:blk0 + nbk], pst[:, 0:nbk * M3].rearrange("p (u c) -> p c u", u=nbk),
                                      r=[("pb", 7)], w=[("uraw", rb)])
                        else:
                            for u in range(nn // 128):
                                blk = b0 // 128 + u
                                pbi = 1 if blk % 2 else 7
                                for kc in range(8):
                                    self.mm(bank(pbi)[:, 0:3 * JI], xbl[xb][:, kc, u * 128:(u + 1) * 128], whc[rb][:, kc, :], kc == 0, kc == 7,
                                            r=[("xbl", xb), ("whc", rb)], w=[("pb", pbi)])
                                self.copy("act", uraw[rb][:, :, blk], bank(pbi)[:, 0:3 * JI], r=[("pb", pbi)], w=[("uraw", rb)])
                    steps.append(step)
                return steps

            for st_ in inproj_steps(0):
                st_()
            for ci in range(G["NCI"]):
                c0 = ci * JI
                rb = ci % 2
                nxt = inproj_steps(ci + 1) if ci + 1 < G["NCI"] else []

                def conv3_piece(sub, t, hh):
                    s0 = sub * Jc
                    rr = sub % 2
                    ca = t * JI + s0 + hh * Jh
                    src = lambda lo: uraw[rb][:, ca:ca + Jh, lo:lo + N1]
                    tp = lambda tap: taps[rb][:, tap, t, s0 + hh * Jh:s0 + (hh + 1) * Jh].unsqueeze(2).to_broadcast([128, Jh, N1])
                    self.tt("dve", cacc[:], src(0), tp(0), ALU.mult, r=[("uraw", rb), ("taps", rb)], w=["cacc"])
                    self.tt("pool", ctmp[:], src(1), tp(1), ALU.mult, r=[("uraw", rb), ("taps", rb)], w=["ctmp"])
                    self.tt("dve", cacc[:], cacc[:], ctmp[:], ALU.add, r=["cacc", "ctmp"], w=["cacc"])
                    self.tt("pool", ctmp[:], src(2), tp(2), ALU.mult, r=[("uraw", rb), ("taps", rb)], w=["ctmp"])
                    self.tt("dve", cacc[:], cacc[:], ctmp[:], ALU.add, r=["cacc", "ctmp"], w=["cacc"])
                    self.tt("dve", xc[rr][t][:, hh * Jh * N1:(hh + 1) * Jh * N1].rearrange("p (c n) -> p c n", n=N1), cacc[:],
                            tbias[rb][:, t, s0 + hh * Jh:s0 + (hh + 1) * Jh].unsqueeze(2).to_broadcast([128, Jh, N1]), ALU.add,
                            r=["cacc", ("taps", rb)], w=[("xc", rr, t)])

                def conv3(sub):
                    for t in range(3):
                        for hh in range(nhh):
                            conv3_piece(sub, t, hh)

                s1_pieces = [(t, hh) for t in range(3) for hh in range(nhh)]

                def kf_load(sc, o, kb):
                    row = (sc * 2 + o) * 128
                    self.dma(kfb[kb][:].rearrange("p a r k -> p (a r k)"), d["kf"][row:row + 128, :], r=["kfscr"], w=[("kfb", kb)])

                flat = [(0, 0), (1, 0), (0, 1), (1, 1)]
                items = []
                kf_load(ci * nsub + flat[0][0], flat[0][1], 0)
                nit = 0
                for ui, (sub, o) in enumerate(flat):
                    kb = ui % 2
                    rr = sub % 2
                    sc = ci * nsub + sub
                    cur = xc[rr][2] if o == 0 else z1[rr]
                    cur_key = ("xc", rr, 2) if o == 0 else ("z1", rr)
                    gate = xc[rr][o]
                    gate_key = ("xc", rr, o)
                    for jp in range(8):
                        cols = slice(jp * 256, (jp + 1) * 256)

                        def x_of(u, ch2, cur=cur, jp=jp):
                            return cur[:, (jp * 2 + u) * 128:(jp * 2 + u + 1) * 128]

                        st, b3, k3 = self.fwd_stages(T, Bf, bank, ii, x_of, 1, [cur_key])
                        ch_lo = sub * Jc + jp * nchan
                        skb = skp[rb][:, o, ch_lo:ch_lo + nchan].unsqueeze(2).to_broadcast([128, nchan, N1])
                        if o == 0:
                            dst, dkey = z1[rr][:, cols], ("z1", rr)
                        else:
                            dst, dkey = z2[:, sub * Jc * N1 + jp * 256:sub * Jc * N1 + (jp + 1) * 256], "z2"
                        st2 = self.inv_stages(T, Bf, bank, ii, b3, k3, kfb[kb][:, jp * 2:(jp + 1) * 2, :, :], ("kfb", kb), cur[:, cols], cur_key,
                                              gate[:, cols], gate_key, skb, dst, dkey, N1, ("skp", rb))
                        ii += 1
                        pre = []
                        if jp == 0 and o == 0 and sub == 0:
                            pre.append(lambda: conv3(0))
                        if o == 0 and sub == 0 and not wstat_dummy:
                            npc = (len(s1_pieces) + 7) // 8
                            for (t_, hh_) in s1_pieces[jp * npc:(jp + 1) * npc]:
                                pre.append(lambda t_=t_, hh_=hh_: conv3_piece(1, t_, hh_))
                        if jp == 5 and ui + 1 < len(flat):
                            nsub_, no_ = flat[ui + 1]
                            pre.append(lambda nsub_=nsub_, no_=no_, kb2=(ui + 1) % 2: kf_load(ci * nsub + nsub_, no_, kb2))
                        lo_, hi_ = nit * len(nxt) // 32, (nit + 1) * len(nxt) // 32
                        pre.extend(nxt[lo_:hi_])
                        nit += 1
                        if pre:
                            f0 = st[0]

                            def F0x(f0=f0, pre=pre):
                                for p_ in pre:
                                    p_()
                                f0()
                            st[0] = F0x
                        items.append(st + st2)
                self.skew_emit(items)
                for c1 in range(0, JI, 128):
                    mz = min(128, JI - c1)
                    z2v = z2[:, c1 * N1:(c1 + mz) * N1].rearrange("p (c n) -> p n c", n=N1)
                    per = 512 // NSEL
                    for nb0 in range(0, N1, per):
                        nn = min(per, N1 - nb0)
                        pb = 4 + (nb0 // per) % 2
                        for i in range(nn):
                            self.mm(bank(pb)[0:mz, i * NSEL:(i + 1) * NSEL], z2v[:, nb0 + i, :], sel[:], True, True, r=["z2", "sel"], w=[("pb", pb)])
                        self.copy("act", ztc[0:mz, :, nb0:nb0 + nn].rearrange("p j n -> p n j"),
                                  bank(pb)[0:mz, 0:nn * NSEL].rearrange("p (n j) -> p n j", j=NSEL), r=[("pb", pb)], w=["ztc"])
                    flat_z = ztc[0:mz, :, :].rearrange("p j n -> p (j n)")
                    self.dma(d["zt"][c0 + c1:c0 + c1 + mz, :], flat_z[:, N1 - 1:N1 - 1 + NQX], r=["ztc"], w=["ztscr"])
            P.barrier()

    def phase_c1(self, G):
        P = self.P
        g, NQX, NQ = G["g"], G["NQX"], G["NQ"]
        d = self.GI[g]
        W = self.W
        w_in = W["w_in"][0]
        with ExitStack() as es:
            y1tv = d["y1t"].rearrange("(kc kp) c -> kp kc c", kp=128)
            lng = self.sb(es, "lng", (128, 1024), F32)
            lnb = self.sb(es, "lnb", (128, 1024), F32)
            xres = [self.sb(es, f"xres{i}", (128, 1024), F32) for i in range(2)]
            pre = self.sb(es, "pre", (128, 1024), F32)
            yo = [self.sb(es, f"yo{i}", (128, 1024), F32) for i in range(2)]
            with ExitStack() as es1:
                ps = self.psum(es1, "psC1", (128, 3584), F32)
                pst = self.psum(es1, "psT", (128, 1024), BF16)
                bank = lambda i: ps[:, i * 512:(i + 1) * 512]
                wg = self.sb(es1, "wg", (128, 8, 2048), BF16)
                wohy = self.sb(es1, "wohy", (128, 8, 1024), BF16)
                womla = self.sb(es1, "womla", (128, 8, 1024), BF16)
                wout = self.sb(es1, "wout", (128, 8, 1024), BF16)
                xs32 = self.sb(es1, "c_xs32", (128, 8, 512), F32)
                xb = self.sb(es1, "c_xb", (128, 8, 512), BF16)
                ztb = self.sb(es1, "c_ztb", (128, 8, 512), BF16)
                otb = self.sb(es1, "c_otb", (128, 8, 512), BF16)
                sg = [self.sb(es1, f"c_sg{i}", (128, 512), F32) for i in range(2)]
                mt_ = [self.sb(es1, f"c_mt{i}", (128, 512), F32) for i in range(2)]
                merged = self.sb(es1, "merged", (128, 8, 512), BF16)
                y1b = self.sb(es1, "y1b", (128, 1024), BF16)
                y1Ts = [self.sb(es1, f"y1Ts{i}", (128, 8, 128), BF16) for i in range(2)]
                self.load_w(wg, w_in[:, IN_G0:IN_G0 + 2048], 8, 2048, "wg")
                self.load_w(wohy, W["w_o_hy"][0], 8, 1024, "wohy")
                self.load_w(womla, W["w_o_mla"][0], 8, 1024, "womla")
                self.load_w(wout, W["w_out"][0], 8, 1024, "wout")
                self.dma(lng[:], W["ln1_g"].partition_broadcast(128), r=[], w=["lnp"])
                self.dma(lnb[:], W["ln1_b"].partition_broadcast(128), r=[], w=["lnp"])
                xqv = d["xqT"].rearrange("(kc kp) c -> kp kc c", kp=128)
                ztv = d["zt"].rearrange("(kc kp) c -> kp kc c", kp=128)
                otv = d["ot"].rearrange("(kc kp) c -> kp kc c", kp=128)
                ti = 0
                for q0 in range(0, NQX, 512):
                    nb = min(512, NQX - q0)
                    self.dma(xs32[:, :, 0:nb], xqv[:, :, q0:q0 + nb], r=[], w=["c_xs32"])
                    self.copy("pool", xb[:, :, 0:nb], xs32[:, :, 0:nb], r=["c_xs32"], w=["c_xb"])
                    self.dma(ztb[:, :, 0:nb], ztv[:, :, q0:q0 + nb], r=["ztscr"], w=["c_ztb"])
                    self.dma(otb[:, :, 0:nb], otv[:, :, q0:q0 + nb], r=["otscr"], w=["c_otb"])
                    for m in range(8):
                        ms = slice(m * 128, (m + 1) * 128)
                        for kc in range(8):
                            self.mm(bank(0)[:, 0:nb], wg[:, kc, m * 128:(m + 1) * 128], xb[:, kc, 0:nb], kc == 0, kc == 7, r=["wg", "c_xb"], w=[("pb", 0)])
                        for kc in range(8):
                            self.mm(bank(1)[:, 0:nb], wohy[:, kc, ms], ztb[:, kc, 0:nb], kc == 0, kc == 7, r=["wohy", "c_ztb"], w=[("pb", 1)])
                        for kc in range(8):
                            self.mm(bank(2)[:, 0:nb], wg[:, kc, 1024 + m * 128:1024 + (m + 1) * 128], xb[:, kc, 0:nb], kc == 0, kc == 7, r=["wg", "c_xb"], w=[("pb", 2)])
                        for kc in range(8):
                            self.mm(bank(3)[:, 0:nb], womla[:, kc, ms], otb[:, kc, 0:nb], kc == 0, kc == 7, r=["womla", "c_otb"], w=[("pb", 3)])
                        self.act(sg[0][:, 0:nb], bank(0)[:, 0:nb], AF.Sigmoid, r=[("pb", 0)], w=[("sg", 0)])
                        self.act(sg[1][:, 0:nb], bank(2)[:, 0:nb], AF.Sigmoid, r=[("pb", 2)], w=[("sg", 1)])
                        self.tt("dve", mt_[0][:, 0:nb], bank(1)[:, 0:nb], sg[0][:, 0:nb], ALU.mult, r=[("pb", 1), ("sg", 0)], w=[("mt", 0)])
                        self.tt("dve", mt_[1][:, 0:nb], bank(3)[:, 0:nb], sg[1][:, 0:nb], ALU.mult, r=[("pb", 3), ("sg", 1)], w=[("mt", 1)])
                        self.tt("pool", merged[:, m, 0:nb], mt_[0][:, 0:nb], mt_[1][:, 0:nb], ALU.add, r=[("mt", 0), ("mt", 1)], w=["merged"])
                    for t0 in range(0, nb, 128):
                        n = min(128, nb - t0)
                        e0 = q0 + t0
                        xr = xres[ti % 2]
                        xk = ("xres", ti % 2)
                        ti += 1
                        self.dma(xr[0:n, :], d["xq"][e0:e0 + n, :], r=[], w=[xk])
                        for hf in range(2):
                            for m in range(8):
                                self.mm(bank(4 + hf)[0:n, :], merged[:, m, t0:t0 + n], wout[:, m, hf * 512:(hf + 1) * 512], m == 0, m == 7,
                                        r=["merged", "wout"], w=[("pb", 4 + hf)])
                        self.stt("dve", pre[0:n, :], xr[0:n, :], ALPHA, ps[0:n, 2048:3072], ALU.mult, ALU.add, r=[xk, ("pb", 4), ("pb", 5)], w=["pre"])
                        yt = yo[ti % 2]
                        yk = ("yo", ti % 2)
                        self.layer_norm(pre, yt, n, lng, lnb, "pre", yk, None)
                        self.dma(d["y1s"][e0:e0 + n, :], yt[0:n, :], r=[yk], w=["y1scr"])
                        self.copy("act", y1b[0:n, :], yt[0:n, :], r=[yk], w=["y1b"])
                        for c in range(8):
                            P.op("pe", lambda e, c=c, n=n: e.transpose(pst[:, c * 128:c * 128 + n], y1b[0:n, c * 128:(c + 1) * 128], self.identb[0:n, 0:n]),
                                 r=["y1b", "identb"], w=["pst"])
                        ys = ti % 2
                        self.copy("dve", y1Ts[ys][:, :, 0:n], pst[:].rearrange("p (c t) -> p c t", c=8)[:, :, 0:n], r=["pst"], w=[("y1Ts", ys)])
                        self.dma(y1tv[:, :, e0:e0 + n], y1Ts[ys][:, :, 0:n], r=[("y1Ts", ys)], w=["y1tscr"])
                P.barrier()

            with ExitStack() as es2:
                ps = self.psum(es2, "psC2", (128, 4096), F32)
                bank = lambda i: ps[:, i * 512:(i + 1) * 512]
                wdn = self.sb(es2, "wdn", (128, NF, 1024), BF16)
                wup = [self.sb(es2, f"wup{i}", (128, 8, 256), BF16) for i in range(2)]
                dww = self.sb(es2, "dww", (128, 3, NF), F32)
                dwb = self.sb(es2, "dwb", (128, NF), F32)
                hm = self.sb(es2, "hm", (128, 2), F32)
                aext = [self.sb(es2, f"aext{i}", (128, 514), F32) for i in range(2)]
                cv = self.sb(es2, "cv", (128, 512), F32)
                gl = self.sb(es2, "gl", (128, 512), F32)
                hmid = self.sb(es2, "hmid", (128, NF, 512), BF16)
                y1Tb = [self.sb(es2, f"y1Tb{i}", (128, 8, 514), BF16) for i in range(2)]
                self.dma(wdn[:], self.wdnb.rearrange("(f p) c -> p f c", p=128), r=["wscr"], w=["wdn"])
                for t in range(3):
                    self.dma(dww[:, t, :], W["dw_w"][0][t].rearrange("(f p) -> p f", p=128), r=[], w=["dww"], slow=True)
                self.dma(dwb[:], W["dw_b"][0].rearrange("(f p) -> p f", p=128), r=[], w=["dww"], slow=True)
                self.dma(hm[:], d["hm"], r=[], w=["hm"])
                self.dma(lng[:], W["ln2_g"].partition_broadcast(128), r=["lnp"], w=["lnp"])
                self.dma(lnb[:], W["ln2_b"].partition_broadcast(128), r=["lnp"], w=["lnp"])
                wupv = self.wupb.rearrange("(kc kp) c -> kp kc c", kp=128)
                nmt = NQ // 512
                wi = 0
                ti = 0
                for mt in range(nmt):
                    e0 = 0
                    y1T = y1Tb[mt % 2]
                    yTk = ("y1Tb", mt % 2)
                    self.dma(y1T[:], y1tv[:, :, 512 * mt:512 * mt + 514], r=["y1tscr"], w=[yTk])
                    for f in range(NF):
                        wb_ = wi % 2
                        wi += 1
                        self.dma(wup[wb_][:, :, 0:128], wupv[:, :, f * 128:(f + 1) * 128], r=["wscr"], w=[("wup", wb_)])
                        self.dma(wup[wb_][:, :, 128:256], wupv[:, :, DFF + f * 128:DFF + (f + 1) * 128], r=["wscr"], w=[("wup", wb_)])
                        pa = 2 * (f % 2)
                        ae = aext[f % 2]
                        ak = ("aext", f % 2)
                        for kc in range(8):
                            self.mm(bank(pa), wup[wb_][:, kc, 0:128], y1T[:, kc, e0:e0 + 512], kc == 0, kc == 7, r=[("wup", wb_), yTk], w=[("pb", pa)])
                        for kc in range(8):
                            self.mm(bank(pa + 1)[:, 0:2], wup[wb_][:, kc, 0:128], y1T[:, kc, e0 + 512:e0 + 514], kc == 0, kc == 7, r=[("wup", wb_), yTk], w=[("pb", pa + 1)])
                        pbk = 4 + f % 2
                        for kc in range(8):
                            self.mm(bank(pbk), wup[wb_][:, kc, 128:256], y1T[:, kc, e0 + 1:e0 + 513], kc == 0, kc == 7, r=[("wup", wb_), yTk], w=[("pb", pbk)])
                        self.copy("act", ae[:, 0:512], bank(pa), r=[("pb", pa)], w=[ak])
                        self.copy("act", ae[:, 512:514], bank(pa + 1)[:, 0:2], r=[("pb", pa + 1)], w=[ak])
                        if mt == 0:
                            self.ts("dve", ae[:, 0:1], ae[:, 0:1], hm[:, 0:1], None, ALU.mult, None, r=[ak, "hm"], w=[ak])
                        if mt == nmt - 1:
                            self.ts("dve", ae[:, 513:514], ae[:, 513:514], hm[:, 1:2], None, ALU.mult, None, r=[ak, "hm"], w=[ak])
                        self.ts("dve", cv[:], ae[:, 0:512], dww[:, 0, f:f + 1], None, ALU.mult, None, r=[ak, "dww"], w=["cv"])
                        self.stt("dve", cv[:], ae[:, 1:513], dww[:, 1, f:f + 1], cv[:], ALU.mult, ALU.add, r=[ak, "dww", "cv"], w=["cv"])
                        self.stt("dve", cv[:], ae[:, 2:514], dww[:, 2, f:f + 1], cv[:], ALU.mult, ALU.add, r=[ak, "dww", "cv"], w=["cv"])
                        self.act(gl[:], cv[:], AF.Gelu, r=["cv", "dww"], w=["gl"], bias=dwb[:, f:f + 1])
                        self.tt("dve", hmid[:, f, :], bank(pbk), gl[:], ALU.mult, r=[("pb", pbk), "gl"], w=["hmid"])
                    for tt_ in range(4):
                        tok0 = 512 * mt + tt_ * 128
                        xr = xres[ti % 2]
                        xk = ("xres", ti % 2)
                        ti += 1
                        self.dma(xr[:], d["y1s"][tok0 + 1:tok0 + 129, :], r=["y1scr"], w=[xk])
                        for hf in range(2):
                            for f in range(NF):
                                self.mm(bank(6 + hf), hmid[:, f, tt_ * 128:(tt_ + 1) * 128], wdn[:, f, hf * 512:(hf + 1) * 512], f == 0, f == NF - 1,
                                        r=["hmid", "wdn"], w=[("pb", 6 + hf)])
                        self.stt("dve", pre[:], xr[:], ALPHA, ps[:, 3072:4096], ALU.mult, ALU.add, r=[xk, ("pb", 6), ("pb", 7)], w=["pre"])
                        yt = yo[ti % 2]
                        yk = ("yo", ti % 2)
                        self.layer_norm(pre, yt, 128, lng, lnb, "pre", yk, None)
                        self.dma(d["y"][tok0:tok0 + 128, :], yt[:], r=[yk], w=["yout"])
                P.barrier()


_CACHE = {}


def _host_inputs(inputs, groups=(0, 1)):
    xs = [np.asarray(inputs["x_sample"], np.float32), np.asarray(inputs["x_prompt"], np.float32)]
    wnames = ["w_in", "short_w", "short_b", "q_norm_g", "w_uq", "kv_norm_g", "w_ukv", "w_o_mla", "filt_w1", "filt_b1", "filt_freq",
              "filt_w2", "filt_b2", "filt_w3", "hy_skip", "w_o_hy", "w_out", "ln1_g", "ln1_b", "w_ffn_up", "dw_w", "dw_b", "w_ffn_down",
              "ln2_g", "ln2_b"]
    base = {n: np.ascontiguousarray(np.asarray(inputs[n], np.float32)) for n in wnames}
    base["c_ident"] = np.eye(128, dtype=np.float32)
    st = np.zeros((128, 128), np.float32)
    for i in range(128):
        st[i, i % 64] = 1.0
        st[i, 64 + i % 64] = 1.0
    base["c_stack2"] = st
    gconst = {}
    for G in GROUPS:
        g, L, N1 = G["g"], G["L"], G["N1"]
        ft = _fft_tables(L, N1)
        zT, deltas, tneg = _filter_tables(L, N1)
        base["c_delta"] = deltas
        for k, v in ft.items():
            base[f"{k}{g}"] = v
        base[f"zemb{g}"] = zT
        base[f"tneg{g}"] = tneg
        tok = (np.arange(N1)[:, None] + N1 * np.arange(128)[None, :]).reshape(-1)
        base[f"ropek{g}"] = _rope_tables(tok)
    in_maps = []
    for core in range(8):
        m = dict(base)
        for G in GROUPS:
            g, L, N1, NQ, NQX, NB, NSEL = G["g"], G["L"], G["N1"], G["NQ"], G["NQX"], G["NB"], G["NSEL"]
            if g == 0:
                seq, t0 = core // 4, NQ * (core % 4)
            else:
                seq, t0 = core // 2, NQ * (core % 2)
            x = xs[g][seq]
            xpad = np.zeros((L + 2, 1024), np.float32)
            xpad[1:L + 1] = x
            idx = (np.arange(-1, N1 + 1)[:, None] + N1 * np.arange(128)[None, :]).reshape(-1) + 1
            m[f"xtp{g}"] = np.ascontiguousarray(xpad[idx].T)
            ext = np.arange(t0 - 1, t0 + NQ + 1)
            xq = xpad[ext + 1]
            m[f"xq{g}"] = np.ascontiguousarray(xq)
            m[f"xqT{g}"] = np.ascontiguousarray(xq.T)
            m[f"ropeq{g}"] = _rope_tables(np.clip(ext, 0, L - 1))
            sel = np.zeros((128, NSEL), np.float32)
            P0 = t0 // N1
            for j in range(NSEL):
                p = P0 - 1 + j
                if 0 <= p < 128:
                    sel[p, j] = 1.0
            m[f"sel{g}"] = sel
            hm = np.zeros((128, 2), np.float32)
            hm[:, 0] = 1.0 if t0 > 0 else 0.0
            hm[:, 1] = 1.0 if t0 + NQ < L else 0.0
            m[f"hm{g}"] = hm
        in_maps.append(m)
    return in_maps


def kernel(**inputs):
    if "nc" not in _CACHE:
        b = Builder()
        _CACHE["nc"] = b.build()
        _CACHE["b"] = b
    nc = _CACHE["nc"]
    in_maps = _host_inputs(inputs)
    names = set(_CACHE["b"].din.keys())
    in_maps = [{k: v for k, v in m.items() if k in names} for m in in_maps]
    res = run_bass_kernel_spmd(nc, in_maps, core_ids=list(range(8)))
    y_sample = np.zeros((2, 16384, 1024), np.float32)
    y_prompt = np.zeros((4, 4096, 1024), np.float32)
    for core in range(8):
        r = res.results[core]
        y_sample[core // 4, 4096 * (core % 4):4096 * (core % 4 + 1)] = r["y0"]
        y_prompt[core // 2, 2048 * (core % 2):2048 * (core % 2 + 1)] = r["y1"]
    return (y_prompt, y_sample)
```
